# Optimizing a Trainium2 kernel written in Bass

```python
import math
import jax, jax.numpy as jnp
from jax import lax
import numpy as np

D_MODEL = 2048
BATCH = 4
SEQ = 4096
DEPTH = 4

GRID_W = 64
ROPE_THETA = 10000.0
Q_BLOCK = 128
LN_EPS = 1e-5
RMS_EPS = 1e-6

DEEPNORM_ALPHA = (2.0 * DEPTH) ** 0.25
DEEPNORM_BETA = (8.0 * DEPTH) ** -0.25

MLA_HEADS = 8
MLA_Q_RANK = 512
MLA_KV_RANK = 256
MLA_NOPE_DIM = 128
MLA_ROPE_DIM = 64
MLA_V_DIM = 128

GLA_HEADS = 4
GLA_DK = 128
GLA_DV = 256
GLA_GATE_RANK = 16
GLA_GATE_TAU = 16.0
GLA_CHUNK = 64

IN_SIZES = (
    MLA_Q_RANK,
    MLA_KV_RANK,
    MLA_ROPE_DIM,
    GLA_HEADS * GLA_DK,
    GLA_HEADS * GLA_DK,
    GLA_HEADS * GLA_DV,
    GLA_HEADS * GLA_DV,
    2 * GLA_GATE_RANK,
)
IN_WIDTH = sum(IN_SIZES)
MIX_WIDTH = MLA_HEADS * MLA_V_DIM + GLA_HEADS * GLA_DV

GQA_HEADS = 16
GQA_KV_HEADS = 4
GQA_HEAD_DIM = 128

N_EXPERTS = 16
EXPERT_FF = 1536
CAPACITY_FACTOR = 2

kernel_name = "hybrid_mla_gla_gqa_ecmoe_deepnorm"


def rms_norm(x, g):
    xf = x.astype(jnp.float32)
    y = xf * lax.rsqrt(jnp.mean(xf * xf, axis=-1, keepdims=True) + RMS_EPS)
    return (y * g.astype(jnp.float32)).astype(x.dtype)


def layer_norm(x, g, b):
    xf = x.astype(jnp.float32)
    mu = jnp.mean(xf, axis=-1, keepdims=True)
    var = jnp.mean(jnp.square(xf - mu), axis=-1, keepdims=True)
    y = (xf - mu) * lax.rsqrt(var + LN_EPS) * g.astype(jnp.float32) + b.astype(jnp.float32)
    return y.astype(x.dtype)


def axial_rope_tables(seq_len, rot_dim):
    rows = seq_len // GRID_W
    row = jnp.repeat(jnp.arange(rows, dtype=jnp.int32), GRID_W).astype(jnp.float32)
    col = jnp.tile(jnp.arange(GRID_W, dtype=jnp.int32), rows).astype(jnp.float32)
    half = rot_dim // 2
    inv_freq = ROPE_THETA ** (-jnp.arange(0, half, 2, dtype=jnp.float32) / half)
    ang = jnp.concatenate([row[:, None] * inv_freq, col[:, None] * inv_freq], axis=-1)
    return jnp.cos(ang), jnp.sin(ang)


def apply_rope(x, cos, sin):
    B, S, H, R = x.shape
    xf = x.astype(jnp.float32).reshape(B, S, H, R // 2, 2)
    c, s = cos[None, :, None, :], sin[None, :, None, :]
    x1, x2 = xf[..., 0], xf[..., 1]
    out = jnp.stack([x1 * c - x2 * s, x1 * s + x2 * c], axis=-1)
    return out.reshape(B, S, H, R).astype(x.dtype)


def block_attention(q, k, v, scale):
    B, S, KVH, G, Dq = q.shape
    Dv = v.shape[-1]
    nb = S // Q_BLOCK
    qb = q.reshape(B, nb, Q_BLOCK, KVH, G, Dq).transpose(1, 0, 2, 3, 4, 5)

    def one_block(q_blk):
        s = jnp.einsum('bqhgd,bkhd->bhgqk', q_blk, k,
                       preferred_element_type=jnp.float32) * scale
        p = jax.nn.softmax(s, axis=-1)
        return jnp.einsum('bhgqk,bkhd->bqhgd', p.astype(v.dtype), v)

    o = lax.map(one_block, qb)
    return o.transpose(1, 0, 2, 3, 4, 5).reshape(B, S, KVH, G, Dv)


def gla_chunked(q, k, v, log_a):
    B, S, H, Dk = q.shape
    Dv = v.shape[-1]
    C = GLA_CHUNK
    N = S // C
    q = q.reshape(B, N, C, H, Dk)
    k = k.reshape(B, N, C, H, Dk)
    v = v.reshape(B, N, C, H, Dv)
    b = jnp.cumsum(log_a.reshape(B, N, C, H, Dk), axis=2)
    b_last = b[:, :, -1]
    q_in = q * jnp.exp(b)
    k_in = k * jnp.exp(-b)
    k_st = k * jnp.exp(b_last[:, :, None] - b)
    mask = jnp.tril(jnp.ones((C, C), dtype=bool))
    att = jnp.einsum('bnihd,bnjhd->bnhij', q_in, k_in)
    att = jnp.where(mask, att, 0.0)
    o_intra = jnp.einsum('bnhij,bnjhv->bnihv', att, v)
    dS = jnp.einsum('bnjhd,bnjhv->bnhdv', k_st, v)
    decay = jnp.exp(b_last)

    def step(s_prev, inp):
        ds_n, dec_n = inp
        return dec_n[..., None] * s_prev + ds_n, s_prev

    s0 = jnp.zeros((B, H, Dk, Dv), jnp.float32)
    _, s_before = lax.scan(step, s0, (dS.transpose(1, 0, 2, 3, 4), decay.transpose(1, 0, 2, 3)))
    s_before = s_before.transpose(1, 0, 2, 3, 4)
    o_inter = jnp.einsum('bnihd,bnhdv->bnihv', q_in, s_before)
    return (o_intra + o_inter).reshape(B, S, H, Dv)


def mla_gla_mixer(x, w_in, q_norm, w_uq, kv_norm, w_ukv, gate_w2, gate_b, gla_norm, w_out,
                  cos_r, sin_r):
    B, S, _ = x.shape
    h = x @ w_in
    offsets = np.cumsum(IN_SIZES)[:-1].tolist()
    c_q, c_kv, k_rope, g_q, g_k, g_v, g_r, g_lat = jnp.split(h, offsets, axis=-1)

    q = (rms_norm(c_q, q_norm) @ w_uq).reshape(B, S, MLA_HEADS, MLA_NOPE_DIM + MLA_ROPE_DIM)
    q_nope, q_pe = q[..., :MLA_NOPE_DIM], q[..., MLA_NOPE_DIM:]
    q_pe = apply_rope(q_pe, cos_r, sin_r)
    kv = (rms_norm(c_kv, kv_norm) @ w_ukv).reshape(B, S, MLA_HEADS, MLA_NOPE_DIM + MLA_V_DIM)
    k_nope, v_mla = kv[..., :MLA_NOPE_DIM], kv[..., MLA_NOPE_DIM:]
    k_pe = apply_rope(k_rope[:, :, None, :], cos_r, sin_r)
    k_pe = jnp.broadcast_to(k_pe, (B, S, MLA_HEADS, MLA_ROPE_DIM))
    q_full = jnp.concatenate([q_nope, q_pe], axis=-1)[:, :, :, None, :]
    k_full = jnp.concatenate([k_nope, k_pe], axis=-1)
    o_mla = block_attention(q_full, k_full, v_mla, (MLA_NOPE_DIM + MLA_ROPE_DIM) ** -0.5)
    o_mla = o_mla.reshape(B, S, MLA_HEADS * MLA_V_DIM)

    f32 = jnp.float32
    gq = g_q.astype(f32).reshape(B, S, GLA_HEADS, GLA_DK) * (GLA_DK ** -0.5)
    gk = g_k.astype(f32).reshape(B, S, GLA_HEADS, GLA_DK)
    gv = g_v.astype(f32).reshape(B, S, GLA_HEADS, GLA_DV)
    lat_f, lat_b = g_lat[..., :GLA_GATE_RANK], g_lat[..., GLA_GATE_RANK:]
    la_f = jax.nn.log_sigmoid((lat_f @ gate_w2[0] + gate_b[0]).astype(f32)) / GLA_GATE_TAU
    la_b = jax.nn.log_sigmoid((lat_b @ gate_w2[1] + gate_b[1]).astype(f32)) / GLA_GATE_TAU
    la_f = la_f.reshape(B, S, GLA_HEADS, GLA_DK)
    la_b = la_b.reshape(B, S, GLA_HEADS, GLA_DK)
    o_fwd = gla_chunked(gq, gk, gv, la_f)
    flip = lambda a: jnp.flip(a, axis=1)
    o_bwd = flip(gla_chunked(flip(gq), flip(gk), flip(gv), flip(la_b)))
    o_gla = rms_norm(o_fwd + o_bwd, gla_norm.reshape(GLA_HEADS, GLA_DV))
    o_gla = o_gla.reshape(B, S, GLA_HEADS * GLA_DV).astype(x.dtype) * jax.nn.silu(g_r)

    return jnp.concatenate([o_mla, o_gla], axis=-1) @ w_out


def gqa_axial_mixer(x, w_qkv, q_norm, k_norm, w_out, cos_g, sin_g):
    B, S, _ = x.shape
    h = x @ w_qkv
    nq = GQA_HEADS * GQA_HEAD_DIM
    nk = GQA_KV_HEADS * GQA_HEAD_DIM
    q = h[..., :nq].reshape(B, S, GQA_HEADS, GQA_HEAD_DIM)
    k = h[..., nq:nq + nk].reshape(B, S, GQA_KV_HEADS, GQA_HEAD_DIM)
    v = h[..., nq + nk:].reshape(B, S, GQA_KV_HEADS, GQA_HEAD_DIM)
    q = apply_rope(rms_norm(q, q_norm), cos_g, sin_g)
    k = apply_rope(rms_norm(k, k_norm), cos_g, sin_g)
    q = q.reshape(B, S, GQA_KV_HEADS, GQA_HEADS // GQA_KV_HEADS, GQA_HEAD_DIM)
    o = block_attention(q, k, v, GQA_HEAD_DIM ** -0.5).reshape(B, S, nq)
    return o @ w_out


def expert_choice_moe(x, w_router, w1, w3, w2):
    B, T, D = x.shape
    cap = CAPACITY_FACTOR * T // N_EXPERTS
    logits = jnp.einsum('btd,de->bte', x, w_router, preferred_element_type=jnp.float32)
    aff = jax.nn.softmax(logits, axis=-1)
    gate, idx = lax.top_k(aff.transpose(0, 2, 1), cap)
    xs = jax.vmap(lambda xb, ib: xb[ib])(x, idx)
    hdn = jax.nn.silu(jnp.einsum('becd,edf->becf', xs, w1)) * jnp.einsum('becd,edf->becf', xs, w3)
    y = jnp.einsum('becf,efd->becd', hdn, w2) * gate[..., None].astype(x.dtype)
    return jax.vmap(
        lambda yb, ib: jnp.zeros((T, D), yb.dtype).at[ib.reshape(-1)].add(yb.reshape(-1, D))
    )(y, idx)


def setup_inputs(seed: int = 0) -> dict:
    key = jax.random.key(seed)
    ks = jax.random.split(key, 24)
    n_even = (DEPTH + 1) // 2
    n_odd = DEPTH // 2
    D = D_MODEL
    nrm = lambda k, shape, fan_in, s=1.0: jax.random.normal(k, shape, jnp.float32) * (s * fan_in ** -0.5)
    gain = lambda k, shape: 1.0 + 0.02 * jax.random.normal(k, shape, jnp.float32)
    small = lambda k, shape: 0.01 * jax.random.normal(k, shape, jnp.float32)
    return {
        "x": jax.random.normal(ks[0], (BATCH, SEQ, D), jnp.float32),
        "mix_w_in": nrm(ks[1], (n_even, D, IN_WIDTH), D),
        "mla_q_norm": gain(ks[2], (n_even, MLA_Q_RANK)),
        "mla_w_uq": nrm(ks[3], (n_even, MLA_Q_RANK, MLA_HEADS * (MLA_NOPE_DIM + MLA_ROPE_DIM)), MLA_Q_RANK),
        "mla_kv_norm": gain(ks[4], (n_even, MLA_KV_RANK)),
        "mla_w_ukv": nrm(ks[5], (n_even, MLA_KV_RANK, MLA_HEADS * (MLA_NOPE_DIM + MLA_V_DIM)), MLA_KV_RANK),
        "gla_gate_w2": nrm(ks[6], (n_even, 2, GLA_GATE_RANK, GLA_HEADS * GLA_DK), GLA_GATE_RANK),
        "gla_gate_b": small(ks[7], (n_even, 2, GLA_HEADS * GLA_DK)),
        "gla_out_norm": gain(ks[8], (n_even, GLA_HEADS * GLA_DV)),
        "mix_w_out": nrm(ks[9], (n_even, MIX_WIDTH, D), MIX_WIDTH, DEEPNORM_BETA),
        "gqa_w_qkv": nrm(ks[10], (n_odd, D, (GQA_HEADS + 2 * GQA_KV_HEADS) * GQA_HEAD_DIM), D),
        "gqa_q_norm": gain(ks[11], (n_odd, GQA_HEAD_DIM)),
        "gqa_k_norm": gain(ks[12], (n_odd, GQA_HEAD_DIM)),
        "gqa_w_out": nrm(ks[13], (n_odd, GQA_HEADS * GQA_HEAD_DIM, D), GQA_HEADS * GQA_HEAD_DIM, DEEPNORM_BETA),
        "moe_router": nrm(ks[14], (DEPTH, D, N_EXPERTS), D),
        "moe_w1": nrm(ks[15], (DEPTH, N_EXPERTS, D, EXPERT_FF), D),
        "moe_w3": nrm(ks[16], (DEPTH, N_EXPERTS, D, EXPERT_FF), D),
        "moe_w2": nrm(ks[17], (DEPTH, N_EXPERTS, EXPERT_FF, D), EXPERT_FF, DEEPNORM_BETA),
        "ln_mix_g": gain(ks[18], (DEPTH, D)),
        "ln_mix_b": small(ks[19], (DEPTH, D)),
        "ln_ffn_g": gain(ks[20], (DEPTH, D)),
        "ln_ffn_b": small(ks[21], (DEPTH, D)),
    }


def reference(x, mix_w_in, mla_q_norm, mla_w_uq, mla_kv_norm, mla_w_ukv, gla_gate_w2, gla_gate_b,
              gla_out_norm, mix_w_out, gqa_w_qkv, gqa_q_norm, gqa_k_norm, gqa_w_out,
              moe_router, moe_w1, moe_w3, moe_w2, ln_mix_g, ln_mix_b, ln_ffn_g, ln_ffn_b):
    S = x.shape[1]
    cos_r, sin_r = axial_rope_tables(S, MLA_ROPE_DIM)
    cos_g, sin_g = axial_rope_tables(S, GQA_HEAD_DIM)
    for layer in range(DEPTH):
        i = layer // 2
        if layer % 2 == 0:
            m = mla_gla_mixer(x, mix_w_in[i], mla_q_norm[i], mla_w_uq[i], mla_kv_norm[i],
                              mla_w_ukv[i], gla_gate_w2[i], gla_gate_b[i], gla_out_norm[i],
                              mix_w_out[i], cos_r, sin_r)
        else:
            m = gqa_axial_mixer(x, gqa_w_qkv[i], gqa_q_norm[i], gqa_k_norm[i], gqa_w_out[i],
                                cos_g, sin_g)
        x = layer_norm(DEEPNORM_ALPHA * x + m, ln_mix_g[layer], ln_mix_b[layer])
        f = expert_choice_moe(x, moe_router[layer], moe_w1[layer], moe_w3[layer], moe_w2[layer])
        x = layer_norm(DEEPNORM_ALPHA * x + f, ln_ffn_g[layer], ln_ffn_b[layer])
    return x
```

```python
import numpy as np
from contextlib import ExitStack
import concourse.bass as bass
import concourse.mybir as mybir
from concourse.bass_utils import run_bass_kernel_spmd

F32 = mybir.dt.float32
BF16 = mybir.dt.bfloat16
I32 = mybir.dt.int32
U32 = mybir.dt.uint32
ALU = mybir.AluOpType
AF = mybir.ActivationFunctionType
AX = mybir.AxisListType

_ISZ = {F32: 4, BF16: 2, I32: 4, U32: 4}
SAME_ENG_SYNC = True
EPOCH = 20000
NDMA = 40
ARENA_WORDS = 51200


class Buf:
    def __init__(self, name, ap, space):
        self.name = name
        self.ap = ap
        self.space = space
        self.lastw = None
        self.reads = {}

    def __getitem__(self, idx):
        return self.ap[idx]


class K:
    def __init__(self, nc, es):
        self.nc = nc
        self.es = es
        self.lists = {e: [] for e in ("pe", "dve", "act", "pool", "sp")}
        self.cnt = {e: 0 for e in self.lists}
        self.epoch = {e: 0 for e in self.lists}
        self.sems = {}
        self.seen = {e: {} for e in self.lists}
        self.dma_i = 0
        self.dma_val = [0] * NDMA
        self.dma_last = [None] * NDMA
        self.out_handles = []
        self.arena = es.enter_context(nc.sbuf_tensor("arena", [128, ARENA_WORDS], F32))
        self.top = 0
        self.marks = []
        self.freed = []
        self.live = []
        self.psum = []
        for i in range(8):
            t = es.enter_context(nc.psum_tensor(f"psb{i}", [128, 512], F32))
            self.psum.append(Buf(f"psb{i}", t[:, :], "psum"))
        self.nsem = 0

    def _sem(self, key):
        if key not in self.sems:
            self.sems[key] = self.es.enter_context(self.nc.semaphore(f"s{self.nsem}"))
            self.nsem += 1
        return self.sems[key]

    def sb(self, name, shape, dtype, parts=128):
        n = int(np.prod(shape))
        words = (n * _ISZ[dtype] + 3) // 4
        words = (words + 15) // 16 * 16
        lo = self.top
        hi = lo + words
        assert hi <= ARENA_WORDS, f"SBUF arena overflow allocating {name}: {hi}"
        self.top = hi
        ap = self.arena[0:parts, lo:hi].bitcast(dtype)
        ap = ap[:, 0:n]
        if len(shape) > 1:
            names = " ".join(f"d{i}" for i in range(len(shape)))
            kw = {f"d{i}": int(s) for i, s in enumerate(shape)}
            ap = ap.rearrange(f"p ({names}) -> p {names}", **kw)
        b = Buf(name, ap, "sbuf")
        pend = []
        keep = []
        for (flo, fhi, hs) in self.freed:
            if flo < hi and lo < fhi:
                pend.extend(hs)
                if flo < lo:
                    keep.append((flo, lo, hs))
                if fhi > hi:
                    keep.append((hi, fhi, hs))
            else:
                keep.append((flo, fhi, hs))
        self.freed = keep
        for i, h in enumerate(pend):
            b.reads[("inh", i)] = h
        self.live.append((lo, hi, b))
        return b

    def mark(self):
        self.marks.append(self.top)

    def release(self):
        m = self.marks.pop()
        keep = []
        for (lo, hi, b) in self.live:
            if lo >= m:
                hs = list(b.reads.values())
                if b.lastw is not None:
                    hs.append(b.lastw)
                if hs:
                    self.freed.append((lo, hi, hs))
            else:
                keep.append((lo, hi, b))
        self.live = keep
        self.top = m

    def dram(self, name, shape, dtype, kind="Internal"):
        t = self.nc.dram_tensor(name, list(shape), dtype, kind=kind)
        return Buf(name, t.ap(), "dram")

    def _waits(self, eng, reads, writes, extra=()):
        deps = list(extra)
        for b in reads:
            if b.lastw is not None:
                deps.append(b.lastw)
        for b in writes:
            if b.lastw is not None:
                deps.append(b.lastw)
            deps.extend(b.reads.values())
        for (key, val, src) in deps:
            if src == eng and (eng == "pe" or not SAME_ENG_SYNC):
                continue
            if self.seen[eng].get(key, 0) >= val:
                continue
            self.seen[eng][key] = val
            self.lists[eng].append(("wait", key, val))

    def op(self, eng, fn, reads=(), writes=()):
        self._waits(eng, reads, writes)
        if self.cnt[eng] >= EPOCH:
            self.epoch[eng] += 1
            self.cnt[eng] = 0
        self.cnt[eng] += 1
        key = (eng, self.epoch[eng])
        h = (key, self.cnt[eng], eng)
        self.lists[eng].append(("inst", fn, key, 1))
        for b in writes:
            b.lastw = h
            b.reads = {}
        for b in reads:
            if b not in writes:
                b.reads[eng] = h
        return h

    def dma(self, q, out, in_, reads=(), writes=(), indirect=None, **kw):
        i = self.dma_i
        self.dma_i = (i + 1) % NDMA
        extra = [self.dma_last[i]] if self.dma_last[i] is not None else []
        self._waits(q, reads, writes, extra)
        self.dma_val[i] += 16
        key = ("dma", i)
        h = (key, self.dma_val[i], "dma")
        self.dma_last[i] = h
        if indirect is None:
            fn = lambda e, out=out, in_=in_, kw=kw: e.dma_start(out=out, in_=in_, **kw)
        else:
            fn = lambda e, out=out, in_=in_, kw=kw, ind=indirect: e.indirect_dma_start(
                out=out, in_=in_, **ind, **kw)
        self.lists[q].append(("inst", fn, key, 16))
        for b in writes:
            b.lastw = h
            b.reads = {}
            if b.space == "dram":
                self.out_handles.append(h)
        for b in reads:
            if b not in writes:
                b.reads[key] = h
        return h

    def finish(self):
        self._waits("sp", (), (), self.out_handles)
        nc = self.nc
        lists = self.lists
        sems = {k: self._sem(k) for k in
                set(x[2] for l in lists.values() for x in l if x[0] == "inst") |
                set(x[1] for l in lists.values() for x in l if x[0] == "wait")}

        def run(lst, e):
            for it in lst:
                if it[0] == "wait":
                    e.wait_ge(sems[it[1]], it[2])
                else:
                    ins = it[1](e)
                    ins.then_inc(sems[it[2]], it[3])

        with nc.Block() as block:
            @block.tensor
            def _(e):
                run(lists["pe"], e)

            @block.vector
            def _(e):
                run(lists["dve"], e)

            @block.scalar
            def _(e):
                run(lists["act"], e)

            @block.gpsimd
            def _(e):
                run(lists["pool"], e)

            @block.sync
            def _(e):
                run(lists["sp"], e)

    def mm(self, out, lhsT, rhs, start, stop, reads, writes):
        return self.op("pe", lambda e: e.matmul(out, lhsT, rhs, start=start, stop=stop),
                       reads, writes)

    def tr(self, out, in_, ident, reads, writes):
        return self.op("pe", lambda e: e.transpose(out, in_, ident), reads, writes)


T = 4096
D = 2048
NT = T // 128
KC = D // 128
ALPHA = float(8.0 ** 0.25)
LN_EPS = 1e-5
RMS_EPS = 1e-6
NEXP = 16
NOWN = 8
FF = 1536
FC = FF // 128
CAP = 512


def load_consts(k, cd):
    C = {}
    C["identf"] = k.sb("identf", [128], F32)
    C["iota"] = k.sb("iota", [512], F32)
    k.dma("sp", C["identf"][:, :], cd.ap[:, 0:128], [cd], [C["identf"]])
    k.dma("sp", C["iota"][:, :], cd.ap[:, 128:640], [cd], [C["iota"]])
    C["identb"] = k.sb("identb", [128], BF16)
    k.op("dve", lambda e: e.tensor_copy(C["identb"][:, :], C["identf"][:, :]), [C["identf"]], [C["identb"]])
    C["onesb"] = k.sb("onesb", [128], BF16)
    k.op("dve", lambda e: e.memset(C["onesb"][:, :], 1.0), [], [C["onesb"]])
    return C


def prologue(k, C, xa, parts, lnp, xn_out=None, xb_out=None, xT_out=None, router=None):
    do_ln = lnp is not None
    k.mark()
    if do_ln:
        gb = k.sb("ln_g", [D], F32)
        bb = k.sb("ln_b", [D], F32)
        k.dma("sp", gb[:, :], lnp.ap[0], [lnp], [gb])
        k.dma("sp", bb[:, :], lnp.ap[1], [lnp], [bb])
    xa_t = [k.sb(f"xa_t{j}", [D], F32) for j in range(2)]
    pt = [[k.sb(f"p{q}_t{j}", [D], F32) for j in range(2)] for q in range(len(parts))]
    xb_t = [k.sb(f"xb_t{j}", [D], BF16) for j in range(2)]
    xT_t = [k.sb(f"xT_t{j}", [KC, 128], BF16) for j in range(2)] if xT_out is not None else None
    st = [k.sb(f"st{j}", [4, 6], F32) for j in range(2)]
    mv = [k.sb(f"mv{j}", [4], F32) for j in range(2)]
    if router is not None:
        wr = k.sb("wr", [KC, NEXP], F32)
        k.dma("sp", wr[:, :, :], router["wr"].ap, [router["wr"]], [wr])
        xTf = k.sb("xTf", [KC, 128], F32)
        rt = [k.sb(f"rt{j}", [NEXP + 4], F32) for j in range(2)]
        af = [k.sb(f"af{j}", [NEXP], F32) for j in range(2)]
        affT = router["affT"]
    PS = k.psum
    for i in range(NT):
        j = i % 2
        rows = slice(i * 128, (i + 1) * 128)
        xt = xa_t[j]
        k.dma("sp", xt[:, :], xa.ap[rows, :], [xa], [xt])
        for q, p in enumerate(parts):
            k.dma("sp" if q == 0 else "act", pt[q][j][:, :], p.ap[rows, :], [p], [pt[q][j]])
        if do_ln:
            p0 = pt[0][j]
            k.op("dve", lambda e, xt=xt, p0=p0: e.scalar_tensor_tensor(
                out=xt[:, :], in0=xt[:, :], scalar=ALPHA, in1=p0[:, :], op0=ALU.mult, op1=ALU.add), [xt, p0], [xt])
            for q in range(1, len(parts)):
                pq = pt[q][j]
                k.op("dve", lambda e, xt=xt, pq=pq: e.tensor_tensor(
                    out=xt[:, :], in0=xt[:, :], in1=pq[:, :], op=ALU.add), [xt, pq], [xt])
            s_, m_ = st[j], mv[j]
            for c in range(4):
                k.op("dve", lambda e, c=c, xt=xt, s_=s_: e.bn_stats(out=s_[:, c, :], in_=xt[:, c * 512:(c + 1) * 512]),
                     [xt], [s_])
            k.op("dve", lambda e, s_=s_, m_=m_: e.bn_aggr(out=m_[:, 0:2], in_=s_[:, :, :].rearrange("p a b -> p (a b)")),
                 [s_], [m_])
            k.op("dve", lambda e, m_=m_: e.tensor_scalar(out=m_[:, 2:3], in0=m_[:, 1:2], scalar1=LN_EPS, scalar2=None,
                                                         op0=ALU.add), [m_], [m_])
            k.op("act", lambda e, m_=m_: e.activation(out=m_[:, 2:3], in_=m_[:, 2:3], func=AF.Sqrt), [m_], [m_])
            k.op("dve", lambda e, m_=m_: e.reciprocal(out=m_[:, 2:3], in_=m_[:, 2:3]), [m_], [m_])
            k.op("dve", lambda e, m_=m_: e.tensor_scalar(out=m_[:, 3:4], in0=m_[:, 0:1], scalar1=m_[:, 2:3], scalar2=-1.0,
                                                         op0=ALU.mult, op1=ALU.mult), [m_], [m_])
            yb = pt[0][j]
            k.op("act", lambda e, yb=yb, xt=xt, m_=m_: e.activation(
                out=yb[:, :], in_=xt[:, :], func=AF.Identity, scale=m_[:, 2:3], bias=m_[:, 3:4]), [xt, m_], [yb])
            k.op("pool", lambda e, yb=yb: e.tensor_tensor(out=yb[:, :], in0=yb[:, :], in1=gb[:, :], op=ALU.mult),
                 [yb, gb], [yb])
            k.op("dve", lambda e, yb=yb, xt=xt: e.tensor_tensor(out=xt[:, :], in0=yb[:, :], in1=bb[:, :], op=ALU.add),
                 [yb, bb], [xt])
        xn = xt
        if xn_out is not None:
            k.dma("act", xn_out.ap[rows, :], xn[:, :], [xn], [xn_out])
        xbt = xb_t[j]
        k.op("act", lambda e, xbt=xbt, xn=xn: e.activation(out=xbt[:, :], in_=xn[:, :], func=AF.Copy), [xn], [xbt])
        if xb_out is not None:
            k.dma("sp", xb_out.ap[rows, :], xbt[:, :], [xbt], [xb_out])
        if xT_out is not None:
            xTt = xT_t[j]
            for hb in range(2):
                ps = PS[hb]
                psb = ps.ap.bitcast(BF16)
                for c in range(8):
                    kc = hb * 8 + c
                    k.tr(psb[:, c * 128:(c + 1) * 128], xbt[:, kc * 128:(kc + 1) * 128], C["identb"][:, :],
                         [xbt, C["identb"]], [ps])
                k.op("dve" if hb == 0 else "act",
                     (lambda e, xTt=xTt, psb=psb, hb=hb: e.tensor_copy(
                         xTt[:, hb * 8:(hb + 1) * 8, :], psb[:, 0:1024].rearrange("p (c t) -> p c t", c=8)))
                     if hb == 0 else
                     (lambda e, xTt=xTt, psb=psb, hb=hb: e.activation(
                         out=xTt[:, hb * 8:(hb + 1) * 8, :], in_=psb[:, 0:1024].rearrange("p (c t) -> p c t", c=8),
                         func=AF.Copy)),
                     [ps], [xTt])
            k.dma("sp", xT_out.ap[:, :, rows], xTt[:, :, :], [xTt], [xT_out])
        if router is not None:
            for g in range(4):
                ps = PS[2 + (g % 2)]
                for c in range(4):
                    kc = g * 4 + c
                    k.tr(ps[:, c * 128:(c + 1) * 128], xn[:, kc * 128:(kc + 1) * 128], C["identf"][:, :],
                         [xn, C["identf"]], [ps])
                k.op("dve" if g % 2 == 0 else "pool" if False else "dve",
                     lambda e, ps=ps, g=g: e.tensor_copy(
                         xTf[:, g * 4:(g + 1) * 4, :], ps[:, 0:512].rearrange("p (c t) -> p c t", c=4)),
                     [ps], [xTf])
            psl = PS[4]
            for kc in range(KC):
                k.mm(psl[:, 0:NEXP], xTf[:, kc, :], wr[:, kc, :], kc == 0, kc == KC - 1, [xTf, wr], [psl])
            r_ = rt[j]
            a_ = af[j]
            k.op("dve", lambda e, r_=r_, psl=psl: e.tensor_reduce(out=r_[:, 16:17], in_=psl[:, 0:NEXP], axis=AX.X, op=ALU.max),
                 [psl], [r_])
            k.op("dve", lambda e, r_=r_: e.tensor_scalar(out=r_[:, 17:18], in0=r_[:, 16:17], scalar1=-1.0, scalar2=None,
                                                         op0=ALU.mult), [r_], [r_])
            k.op("act", lambda e, r_=r_, psl=psl: e.activation(out=r_[:, 0:NEXP], in_=psl[:, 0:NEXP], func=AF.Exp,
                                                               bias=r_[:, 17:18], scale=1.0, accum_out=r_[:, 18:19]),
                 [psl, r_], [r_])
            k.op("dve", lambda e, r_=r_: e.reciprocal(out=r_[:, 19:20], in_=r_[:, 18:19]), [r_], [r_])
            k.op("dve", lambda e, r_=r_, a_=a_: e.tensor_scalar(out=a_[:, :], in0=r_[:, 0:NEXP], scalar1=r_[:, 19:20],
                                                                scalar2=None, op0=ALU.mult), [r_], [a_])
            k.dma("act", router["aff_out"].ap[rows, :], a_[:, :], [a_], [router["aff_out"]])
            pst = PS[5]
            k.tr(pst[0:NEXP, 0:128], a_[:, :], C["identf"][:, :], [a_, C["identf"]], [pst])
            k.op("dve", lambda e, pst=pst, i=i: e.tensor_copy(affT[0:NEXP, i * 128:(i + 1) * 128], pst[0:NEXP, 0:128]),
                 [pst], [affT])
    k.release()


def moe_body(k, C, affT, aff_d, xb_d, w1_d, w3_d, w2_d, f_out):
    PS = k.psum
    k.mark()
    csT = k.sb("csT", [NT, NEXP], F32)
    k.mark()
    z = k.sb("zero", [D], F32)
    k.op("pool", lambda e: e.memset(z[:, :], 0.0), [], [z])
    for i in range(NT):
        k.dma("sp" if i % 2 == 0 else "act", f_out.ap[i * 128:(i + 1) * 128, :], z[:, :], [z], [f_out])
    junk = k.sb("junk", [T], BF16)
    sm = k.sb("bis", [16], F32)
    R16 = slice(0, NEXP)
    k.op("dve", lambda e: e.memset(sm[R16, :], 0.0), [], [sm])
    k.op("dve", lambda e: e.memset(sm[R16, 1:2], 1.0), [sm], [sm])
    for it in range(30):
        k.op("dve", lambda e: e.tensor_tensor(out=sm[R16, 2:3], in0=sm[R16, 0:1], in1=sm[R16, 1:2], op=ALU.add), [sm], [sm])
        k.op("dve", lambda e: e.tensor_scalar(out=sm[R16, 2:3], in0=sm[R16, 2:3], scalar1=0.5, scalar2=None, op0=ALU.mult),
             [sm], [sm])
        k.op("dve", lambda e: e.tensor_scalar(out=junk[R16, :], in0=affT[R16, :], scalar1=sm[R16, 2:3], scalar2=0.0,
                                              op0=ALU.is_ge, op1=ALU.add, accum_out=sm[R16, 3:4]), [affT, sm], [junk, sm])
        k.op("dve", lambda e: e.tensor_scalar(out=sm[R16, 4:5], in0=sm[R16, 3:4], scalar1=float(CAP) - 0.5, scalar2=None,
                                              op0=ALU.is_ge), [sm], [sm])
        k.op("dve", lambda e: e.tensor_scalar(out=sm[R16, 7:8], in0=sm[R16, 4:5], scalar1=-1.0, scalar2=1.0,
                                              op0=ALU.mult, op1=ALU.add), [sm], [sm])
        k.op("dve", lambda e: e.tensor_tensor(out=sm[R16, 5:6], in0=sm[R16, 2:3], in1=sm[R16, 0:1], op=ALU.subtract),
             [sm], [sm])
        k.op("dve", lambda e: e.tensor_tensor(out=sm[R16, 6:7], in0=sm[R16, 2:3], in1=sm[R16, 1:2], op=ALU.subtract),
             [sm], [sm])
        k.op("dve", lambda e: e.scalar_tensor_tensor(out=sm[R16, 0:1], in0=sm[R16, 5:6], scalar=sm[R16, 4:5],
                                                     in1=sm[R16, 0:1], op0=ALU.mult, op1=ALU.add), [sm], [sm])
        k.op("dve", lambda e: e.scalar_tensor_tensor(out=sm[R16, 1:2], in0=sm[R16, 6:7], scalar=sm[R16, 7:8],
                                                     in1=sm[R16, 1:2], op0=ALU.mult, op1=ALU.add), [sm], [sm])
    mask = k.sb("mask", [T], F32)
    ones = k.sb("onesT", [T], BF16)
    cs = k.sb("cs", [T], F32)
    k.op("dve", lambda e: e.tensor_scalar(out=mask[R16, :], in0=affT[R16, :], scalar1=sm[R16, 0:1], scalar2=None,
                                          op0=ALU.is_ge), [affT, sm], [mask])
    k.op("pool", lambda e: e.memset(ones[R16, :], 1.0), [], [ones])
    k.op("dve", lambda e: e.tensor_tensor_scan(out=cs[R16, :], data0=ones[R16, :], data1=mask[R16, :], initial=0.0,
                                               op0=ALU.mult, op1=ALU.add), [ones, mask], [cs])
    for c in range(NT):
        ps = PS[c % 2]
        k.tr(ps[:, 0:NEXP], cs[R16, c * 128:(c + 1) * 128], C["identf"][R16, 0:NEXP], [cs, C["identf"]], [ps])
        k.op("dve", lambda e, ps=ps, c=c: e.tensor_copy(csT[:, c, :], ps[:, 0:NEXP]), [ps], [csT])
    k.release()
    cmps = [k.sb(f"cmp{j}", [NT, 128], BF16) for j in range(2)]
    idx_i = [k.sb(f"idx{j}", [4], I32) for j in range(2)]
    gt = [k.sb(f"gt{j}", [4, NEXP], F32) for j in range(2)]
    xs = k.sb("xs", [4, D], BF16)
    xsT = [k.sb(f"xsT{j}", [KC, CAP], BF16) for j in range(2)]
    hT = k.sb("hT", [FC, CAP], BF16)
    w1c = [k.sb(f"w1c{j}", [KC, 128], BF16) for j in range(2)]
    w3c = [k.sb(f"w3c{j}", [KC, 128], BF16) for j in range(2)]
    w2b = [k.sb(f"w2b{j}", [FC, 512], BF16) for j in range(4)]
    sg = [k.sb(f"sg{j}", [CAP], F32) for j in range(2)]
    yt = [k.sb(f"yt{j}", [D], F32) for j in range(2)]

    def stageA(e_):
        j = e_ % 2
        psI = PS[2]
        for jb in range(4):
            cmp = cmps[jb % 2]
            for c in range(NT):
                k.op("dve", lambda e, c=c, cmp=cmp, jb=jb: e.tensor_scalar(
                    out=cmp[:, c, :], in0=C["iota"][:, jb * 128:(jb + 1) * 128], scalar1=csT[:, c, e_:e_ + 1],
                    scalar2=None, op0=ALU.is_ge), [C["iota"], csT], [cmp])
            for c in range(NT):
                k.mm(psI[:, jb:jb + 1], cmp[:, c, :], C["onesb"][:, 0:1], c == 0, c == NT - 1,
                     [cmp, C["onesb"]], [psI])
        ii = idx_i[j]
        k.op("dve", lambda e, ii=ii, psI=psI: e.tensor_copy(ii[:, :], psI[:, 0:4]), [psI], [ii])
        g_ = gt[j]
        for jb in range(4):
            k.dma("pool", g_[:, jb, :], aff_d.ap, [aff_d, ii], [g_],
                  indirect=dict(out_offset=None, in_offset=bass.IndirectOffsetOnAxis(ap=ii[:, jb:jb + 1], axis=0)))
        for jb in range(4):
            k.dma("pool", xs[:, jb, :], xb_d.ap, [xb_d, ii], [xs],
                  indirect=dict(out_offset=None, in_offset=bass.IndirectOffsetOnAxis(ap=ii[:, jb:jb + 1], axis=0)))
        xT_ = xsT[j]
        for jb in range(4):
            for hb in range(2):
                ps = PS[hb]
                psb = ps.ap.bitcast(BF16)
                for c in range(8):
                    kc = hb * 8 + c
                    k.tr(psb[:, c * 128:(c + 1) * 128], xs[:, jb, kc * 128:(kc + 1) * 128], C["identb"][:, :],
                         [xs, C["identb"]], [ps])
                if hb == 0:
                    k.op("dve", lambda e, psb=psb, jb=jb, xT_=xT_: e.tensor_copy(
                        xT_[:, 0:8, jb * 128:(jb + 1) * 128], psb[:, 0:1024].rearrange("p (c t) -> p c t", c=8)),
                        [ps], [xT_])
                else:
                    k.op("act", lambda e, psb=psb, jb=jb, xT_=xT_: e.activation(
                        out=xT_[:, 8:16, jb * 128:(jb + 1) * 128], in_=psb[:, 0:1024].rearrange("p (c t) -> p c t", c=8),
                        func=AF.Copy), [ps], [xT_])

    def stageB(e_):
        j = e_ % 2
        xT_ = xsT[j]
        ii = idx_i[j]
        g_ = gt[j]
        for cb in range(4):
            k.dma("pool", w2b[cb][:, :, :],
                  w2_d.ap[e_].rearrange("(fc p) d -> p fc d", p=128)[:, :, cb * 512:(cb + 1) * 512],
                  [w2_d], [w2b[cb]])
        for fc in range(FC):
            q = fc % 2
            k.dma("pool", w1c[q][:, :, :], w1_d.ap[e_, fc], [w1_d], [w1c[q]])
            k.dma("pool", w3c[q][:, :, :], w3_d.ap[e_, fc], [w3_d], [w3c[q]])
            p1 = PS[3 + q]
            p3 = PS[5 + q]
            for kc in range(KC):
                k.mm(p1[:, :], w1c[q][:, kc, :], xT_[:, kc, :], kc == 0, kc == KC - 1, [w1c[q], xT_], [p1])
            for kc in range(KC):
                k.mm(p3[:, :], w3c[q][:, kc, :], xT_[:, kc, :], kc == 0, kc == KC - 1, [w3c[q], xT_], [p3])
            s_ = sg[q]
            k.op("act", lambda e, s_=s_, p1=p1: e.activation(out=s_[:, :], in_=p1[:, :], func=AF.Silu), [p1], [s_])
            k.op("dve", lambda e, s_=s_, p3=p3, fc=fc: e.tensor_tensor(out=hT[:, fc, :], in0=s_[:, :], in1=p3[:, :],
                                                                       op=ALU.mult), [s_, p3], [hT])
        for sbk in range(4):
            y_ = yt[sbk % 2]
            for cb in range(4):
                py = PS[3 + (cb % 2)]
                for fc in range(FC):
                    k.mm(py[:, :], hT[:, fc, sbk * 128:(sbk + 1) * 128], w2b[cb][:, fc, :], fc == 0, fc == FC - 1,
                         [hT, w2b[cb]], [py])
                k.op("act", lambda e, y_=y_, py=py, cb=cb, sbk=sbk, g_=g_: e.activation(
                    out=y_[:, cb * 512:(cb + 1) * 512], in_=py[:, :], func=AF.Copy, scale=g_[:, sbk, e_:e_ + 1]),
                    [py, g_], [y_])
            k.dma("pool", f_out.ap, y_[:, :], [y_, ii], [f_out],
                  indirect=dict(out_offset=bass.IndirectOffsetOnAxis(ap=ii[:, sbk:sbk + 1], axis=0), in_offset=None,
                                compute_op=ALU.add))

    stageA(0)
    for e_ in range(NOWN):
        if e_ + 1 < NOWN:
            stageA(e_ + 1)
        stageB(e_)
    k.release()


def build_moe():
    nc = bass.Bass("TRN2", target_bir_lowering=False)
    es = ExitStack()
    k = K(nc, es)
    cd = k.dram("consts", [128, 640], F32, "ExternalInput")
    xa = k.dram("xa", [T, D], F32, "ExternalInput")
    pa = k.dram("pa", [T, D], F32, "ExternalInput")
    pb = k.dram("pb", [T, D], F32, "ExternalInput")
    lnp = k.dram("lnp", [2, 128, D], F32, "ExternalInput")
    wr = k.dram("wr", [128, KC, NEXP], F32, "ExternalInput")
    w1 = k.dram("w1", [NOWN, FC, 128, KC, 128], F32, "ExternalInput")
    w3 = k.dram("w3", [NOWN, FC, 128, KC, 128], F32, "ExternalInput")
    w2 = k.dram("w2", [NOWN, FF, D], F32, "ExternalInput")
    xmid = k.dram("xmid", [T, D], F32, "ExternalOutput")
    f_out = k.dram("f", [T, D], F32, "ExternalOutput")
    xb_d = k.dram("xb_d", [T, D], BF16, "Internal")
    aff_d = k.dram("aff_d", [T, NEXP], F32, "Internal")
    C = load_consts(k, cd)
    affT = k.sb("affT", [T], F32)
    prologue(k, C, xa, [pa, pb], lnp, xn_out=xmid, xb_out=xb_d, xT_out=None,
             router=dict(wr=wr, aff_out=aff_d, affT=affT))
    moe_body(k, C, affT, aff_d, xb_d, w1, w3, w2, f_out)
    k.finish()
    return nc, es


def consts_np():
    c = np.zeros((128, 640), np.float32)
    c[:, 0:128] = np.eye(128, dtype=np.float32)
    c[:, 128:640] = np.arange(512, dtype=np.float32)[None, :]
    return c


def moe_inputs(layer, h, xa, pa, pb, inputs):
    own = list(range(8 * h, 8 * h + 8))
    oth = [e for e in range(NEXP) if e not in own]
    wr = inputs["moe_router"][layer][:, own + oth]
    wr = np.ascontiguousarray(wr.reshape(KC, 128, NEXP).transpose(1, 0, 2))
    def prep13(w):
        w = w[layer][own]
        w = w.reshape(NOWN, KC, 128, FC, 128).transpose(0, 3, 2, 1, 4)
        return np.ascontiguousarray(w)
    lnp = np.stack([np.broadcast_to(inputs["ln_mix_g"][layer], (128, D)),
                    np.broadcast_to(inputs["ln_mix_b"][layer], (128, D))]).astype(np.float32)
    return {"consts": consts_np(), "xa": xa, "pa": pa, "pb": pb, "lnp": np.ascontiguousarray(lnp), "wr": wr,
            "w1": prep13(inputs["moe_w1"]), "w3": prep13(inputs["moe_w3"]),
            "w2": np.ascontiguousarray(inputs["moe_w2"][layer][own])}


DBG = {}
def attn_head(k, C, kT_parts, qT_parts, vaug_fn, scale, PT, OT_dst, dv=128):
    PS = k.psum
    nk = T // 128
    npart = len(kT_parts)

    def S(kc):
        ps = PS[kc % 2]
        for pi in range(npart):
            kb, kap = kT_parts[pi]
            qb, qap = qT_parts[pi]
            k.mm(ps[:, :], kap[:, kc * 128:(kc + 1) * 128], qap, pi == 0, pi == npart - 1, [kb, qb], [ps])
        pt = PT[kc % 3]
        k.op("act", lambda e, pt=pt, ps=ps: e.activation(out=pt[:, :], in_=ps[:, :], func=AF.Exp, scale=scale), [ps], [pt])

    def PV(kc):
        pt = PT[kc % 3]
        vb, vap = vaug_fn(kc)
        for sub in range(4):
            po = PS[2 + sub]
            k.mm(po[:, 0:dv + 1], pt[:, sub * 128:(sub + 1) * 128], vap, kc == 0, kc == nk - 1, [pt, vb], [po])

    S(0)
    for kc in range(nk):
        if kc + 1 < nk:
            S(kc + 1)
        if not DBG.get("skip_pv"):
            PV(kc)
    for _d in range(DBG.get("pushers", 0)):
        k.mm(PS[2][:, 256:258], C["onesb"][:, :], C["onesb"][:, 0:2], True, True, [], [])
    on = C["on"]
    rinv = C["rinv"]
    if C.get("ON_dst") is not None:
        for sub in range(4):
            po = PS[2 + sub]
            ob, oap = C["ON_dst"](sub)
            k.op("dve", lambda e, po=po, sub=sub: e.reciprocal(out=rinv[:, sub:sub + 1], in_=po[:, dv:dv + 1]), [po], [rinv])
            k.op("dve", lambda e, po=po, sub=sub, oap=oap: e.tensor_scalar(
                out=oap, in0=po[:, 0:dv], scalar1=rinv[:, sub:sub + 1], scalar2=None, op0=ALU.mult), [po, rinv], [ob])
        return
    for sub in range(4 if not DBG.get("skip_epi") else 0):
        po = PS[2 + sub]
        k.op("dve", lambda e, po=po: e.reciprocal(out=rinv[:, 0:1], in_=po[:, dv:dv + 1]), [po], [rinv])
        if DBG.get("epi_actnorm"):
            k.op("act", lambda e, po=po: e.activation(out=on[:, 0:dv], in_=po[:, 0:dv], func=AF.Copy, scale=rinv[:, 0:1]),
                 [po, rinv], [on])
        else:
            k.op("dve", lambda e, po=po: e.tensor_scalar(out=on[:, 0:dv], in0=po[:, 0:dv], scalar1=rinv[:, 0:1], scalar2=None,
                                                         op0=ALU.mult), [po, rinv], [on])
        if DBG.get("epi1"):
            continue
        pst = PS[DBG.get("epi_bank", 7)]
        psb = pst.ap.bitcast(BF16)
        if DBG.get("epi_const"):
            k.tr(psb[0:dv, 0:128], C["identb"][:, 0:dv], C["identb"][:, :], [C["identb"]], [pst])
        else:
            k.tr(psb[0:dv, 0:128], on[:, 0:dv], C["identb"][:, :], [on, C["identb"]], [pst])
        if DBG.get("epi2"):
            continue
        db, dap = OT_dst(sub)
        if DBG.get("epi_dve"):
            k.op("dve", lambda e, dap=dap, psb=psb: e.tensor_copy(dap, psb[0:dv, 0:128]), [pst], [db])
        else:
            k.op("act", lambda e, dap=dap, psb=psb: e.activation(out=dap, in_=psb[0:dv, 0:128], func=AF.Copy), [pst], [db])


def flush_on(k, C, ONb, nheads, OT):
    PS = k.psum
    for hh in range(nheads):
        pst = PS[7]
        psb = pst.ap.bitcast(BF16)
        for sub in range(4):
            if DBG.get("flush_const"):
                k.tr(psb[:, sub * 128:(sub + 1) * 128], C["identb"][:, :], C["identb"][:, :], [ONb, C["identb"]], [pst])
            else:
                k.tr(psb[:, sub * 128:(sub + 1) * 128], ONb[:, hh, sub, :], C["identb"][:, :], [ONb, C["identb"]], [pst])
        k.op("act", lambda e, psb=psb, hh=hh: e.activation(out=OT[:, hh, :], in_=psb[:, 0:512], func=AF.Copy), [pst], [OT])


def rope_tm(k, src, nh, npairs, cos_t, sin_t, dst, tmp):
    sb_, sap = src
    cb_, cap_ = cos_t
    nb_, nap = sin_t
    db_, dap = dst
    n = nh * npairs
    ev = sap.rearrange("p (n two) -> p n two", two=2)[:, :, 0]
    od = sap.rearrange("p (n two) -> p n two", two=2)[:, :, 1]
    dev = dap.rearrange("p (n two) -> p n two", two=2)[:, :, 0]
    dod = dap.rearrange("p (n two) -> p n two", two=2)[:, :, 1]
    t1, t2 = tmp
    k.op("dve", lambda e: e.tensor_tensor(out=t1[:, 0:n], in0=ev, in1=cap_, op=ALU.mult), [sb_, cb_], [t1])
    k.op("pool", lambda e: e.tensor_tensor(out=t2[:, 0:n], in0=od, in1=nap, op=ALU.mult), [sb_, nb_], [t2])
    k.op("dve", lambda e: e.tensor_tensor(out=dev, in0=t1[:, 0:n], in1=t2[:, 0:n], op=ALU.subtract), [t1, t2], [db_])
    k.op("dve", lambda e: e.tensor_tensor(out=t1[:, 0:n], in0=ev, in1=nap, op=ALU.mult), [sb_, nb_], [t1])
    k.op("pool", lambda e: e.tensor_tensor(out=t2[:, 0:n], in0=od, in1=cap_, op=ALU.mult), [sb_, cb_], [t2])
    k.op("dve", lambda e: e.tensor_tensor(out=dod, in0=t1[:, 0:n], in1=t2[:, 0:n], op=ALU.add), [t1, t2], [db_])


def headnorm_tm(k, ps_list, nh, gtab, qn, sq, rr):
    for i, (pb_, pap) in enumerate(ps_list):
        w = min(512, nh * 128 - i * 512)
        k.op("act", lambda e, pap=pap, i=i, w=w: e.activation(out=sq[:, i * 512:i * 512 + w], in_=pap[:, 0:w], func=AF.Square),
             [pb_], [sq])
    k.op("dve", lambda e: e.tensor_reduce(out=rr[:, 0:nh], in_=sq[:, 0:nh * 128].rearrange("p (h d) -> p h d", d=128),
                                          axis=AX.X, op=ALU.add), [sq], [rr])
    k.op("dve", lambda e: e.tensor_scalar(out=rr[:, 0:nh], in0=rr[:, 0:nh], scalar1=1.0 / 128, scalar2=RMS_EPS,
                                          op0=ALU.mult, op1=ALU.add), [rr], [rr])
    k.op("act", lambda e: e.activation(out=rr[:, 0:nh], in_=rr[:, 0:nh], func=AF.Sqrt), [rr], [rr])
    k.op("dve", lambda e: e.reciprocal(out=rr[:, 0:nh], in_=rr[:, 0:nh]), [rr], [rr])
    for h in range(nh):
        pb_, pap = ps_list[h // 4]
        c0 = (h % 4) * 128
        k.op("dve", lambda e, h=h, pap=pap, c0=c0: e.scalar_tensor_tensor(
            out=qn[:, h * 128:(h + 1) * 128], in0=pap[:, c0:c0 + 128], scalar=rr[:, h:h + 1], in1=gtab[:, :],
            op0=ALU.mult, op1=ALU.mult), [pb_, rr, gtab], [qn])


def load_w_cast(k, dst, src_ap, src_buf, nsplit=1):
    a = dst.ap.shape[1]
    step = (a + nsplit - 1) // nsplit
    for s0 in range(0, a, step):
        k.dma("pool", dst[:, s0:s0 + step, :], src_ap[:, s0:s0 + step, :], [src_buf], [dst])


def gqa_body(k, C, xT_d, wq_d, wkv_d, wo_d, gq_d, gk_d, cos_d, sin_d, m_out):
    PS = k.psum
    k.mark()
    KT = [k.sb(f"KT{h}", [T], BF16) for h in range(2)]
    VA = k.sb("VA", [NT, 2, 130], BF16)
    C["on"] = k.sb("on", [128], BF16)
    C["rinv"] = k.sb("rinv", [4], F32)
    gq = k.sb("gq", [128], F32)
    gk = k.sb("gk", [128], F32)
    k.dma("sp", gq[:, :], gq_d.ap, [gq_d], [gq])
    k.dma("sp", gk[:, :], gk_d.ap, [gk_d], [gk])
    xTb = k.sb("xTb", [KC, 512], BF16)
    cos_t = [k.sb(f"cos{j}", [512], F32) for j in range(2)]
    sin_t = [k.sb(f"sin{j}", [512], F32) for j in range(2)]
    sq = k.sb("sq", [1024], F32)
    qn = k.sb("qn", [1024], F32)
    rr = k.sb("rr", [8], F32)
    tmp = [k.sb(f"rtmp{j}", [512], F32) for j in range(2)]
    qr = k.sb("qr", [1024], BF16)
    k.op("pool", lambda e: e.memset(VA[:, :, :, 128:130], 1.0), [], [VA])
    k.mark()
    wkv = k.sb("wkv", [KC, 512], BF16)
    load_w_cast(k, wkv, wkv_d.ap, wkv_d, 2)
    for tb in range(8):
        k.dma("sp", xTb[:, :, :], xT_d.ap[:, :, tb * 512:(tb + 1) * 512], [xT_d], [xTb])
        for tl in range(4):
            ti = tb * 4 + tl
            j = ti % 2
            rows = slice(ti * 128, (ti + 1) * 128)
            k.dma("sp", cos_t[j][:, 0:128], cos_d.ap[rows, 0:128], [cos_d], [cos_t[j]])
            k.dma("sp", sin_t[j][:, 0:128], sin_d.ap[rows, 0:128], [sin_d], [sin_t[j]])
            ps = PS[6]
            for kc in range(KC):
                k.mm(ps[:, :], xTb[:, kc, tl * 128:(tl + 1) * 128], wkv[:, kc, :], kc == 0, kc == KC - 1, [xTb, wkv], [ps])
            k.op("act", lambda e, ps=ps, ti=ti: e.activation(
                out=VA[:, ti, :, 0:128], in_=ps[:, 256:512].rearrange("p (h d) -> p h d", h=2), func=AF.Copy), [ps], [VA])
            headnorm_tm(k, [(ps, ps[:, 0:256])], 2, gk, qn, sq, rr)
            rope_tm(k, (qn, qn[:, 0:256]), 2, 64, (cos_t[j], cos_t[j][:, 0:128]), (sin_t[j], sin_t[j][:, 0:128]),
                    (qr, qr[:, 0:256]), tmp)
            pst = PS[7]
            psb = pst.ap.bitcast(BF16)
            for h in range(2):
                k.tr(psb[:, h * 128:(h + 1) * 128], qr[:, h * 128:(h + 1) * 128], C["identb"][:, :], [qr, C["identb"]], [pst])
            for h in range(2):
                k.op("act", lambda e, h=h, psb=psb, rows=rows: e.activation(
                    out=KT[h][:, rows], in_=psb[:, h * 128:(h + 1) * 128], func=AF.Copy), [pst], [KT[h]])
    k.release()
    wq = k.sb("wq", [KC, 1024], BF16)
    wo = k.sb("wo", [8, D], BF16)
    load_w_cast(k, wq, wq_d.ap, wq_d, 4)
    load_w_cast(k, wo, wo_d.ap, wo_d, 4)
    QT = k.sb("QT", [8, 512], BF16)
    OT = k.sb("OT", [8, 512], BF16)
    ONb = k.sb("ONb", [8, 4, 128], BF16)
    PT = [k.sb(f"PT{j}", [512], BF16) for j in range(3)]
    mt = [k.sb(f"mt{j}", [D], F32) for j in range(2)]
    scale = float(128 ** -0.5)
    for qb in range(8):
        k.dma("sp", xTb[:, :, :], xT_d.ap[:, :, qb * 512:(qb + 1) * 512], [xT_d], [xTb])
        for tl in range(4):
            ti = qb * 4 + tl
            j = ti % 2
            rows = slice(ti * 128, (ti + 1) * 128)
            k.dma("sp", cos_t[j][:, :], cos_d.ap[rows, :], [cos_d], [cos_t[j]])
            k.dma("sp", sin_t[j][:, :], sin_d.ap[rows, :], [sin_d], [sin_t[j]])
            pss = [PS[6], PS[7]]
            for g in range(2):
                for kc in range(KC):
                    k.mm(pss[g][:, :], xTb[:, kc, tl * 128:(tl + 1) * 128], wq[:, kc, g * 512:(g + 1) * 512],
                         kc == 0, kc == KC - 1, [xTb, wq], [pss[g]])
            headnorm_tm(k, [(pss[0], pss[0][:, :]), (pss[1], pss[1][:, :])], 8, gq, qn, sq, rr)
            rope_tm(k, (qn, qn[:, :]), 8, 64, (cos_t[j], cos_t[j][:, :]), (sin_t[j], sin_t[j][:, :]), (qr, qr[:, :]), tmp)
            pst = PS[7]
            psb = pst.ap.bitcast(BF16)
            for h in range(8):
                k.tr(psb[:, h * 128:(h + 1) * 128], qr[:, h * 128:(h + 1) * 128], C["identb"][:, :], [qr, C["identb"]], [pst])
            k.op("act", lambda e, psb=psb, tl=tl: e.activation(
                out=QT[:, :, tl * 128:(tl + 1) * 128], in_=psb[:, 0:1024].rearrange("p (h t) -> p h t", h=8), func=AF.Copy),
                [pst], [QT])
        for hq in range(8):
            kvh = hq // 4
            C["ON_dst"] = (lambda sub, hq=hq: (ONb, ONb[:, hq, sub, :]))
            attn_head(k, C, [(KT[kvh], KT[kvh][:, :])], [(QT, QT[:, hq, :])],
                      lambda kc, kvh=kvh: (VA, VA[:, kc, kvh, 0:129]), scale, PT,
                      lambda sub, hq=hq: (OT, OT[:, hq, sub * 128:(sub + 1) * 128]))
        C["ON_dst"] = None
        flush_on(k, C, ONb, 8, OT)
        for tl in range(4):
            ti = qb * 4 + tl
            m_ = mt[ti % 2]
            for cb in range(4):
                ps = PS[6]
                for hq in range(8):
                    k.mm(ps[:, :], OT[:, hq, tl * 128:(tl + 1) * 128], wo[:, hq, cb * 512:(cb + 1) * 512],
                         hq == 0, hq == 7, [OT, wo], [ps])
                k.op("dve", lambda e, m_=m_, ps=ps, cb=cb: e.tensor_copy(m_[:, cb * 512:(cb + 1) * 512], ps[:, :]), [ps], [m_])
            k.dma("sp", m_out.ap[ti * 128:(ti + 1) * 128, :], m_[:, :], [m_], [m_out])
    k.release()


def build_gqa(first=False):
    nc = bass.Bass("TRN2", target_bir_lowering=False)
    es = ExitStack()
    k = K(nc, es)
    cd = k.dram("consts", [128, 640], F32, "ExternalInput")
    xa = k.dram("xa", [T, D], F32, "ExternalInput")
    pa = k.dram("pa", [T, D], F32, "ExternalInput")
    pb = k.dram("pb", [T, D], F32, "ExternalInput")
    lnp = k.dram("lnp", [2, 128, D], F32, "ExternalInput")
    wq = k.dram("wq", [128, KC, 1024], F32, "ExternalInput")
    wkv = k.dram("wkv", [128, KC, 512], F32, "ExternalInput")
    wo = k.dram("wo", [128, 8, D], F32, "ExternalInput")
    gq = k.dram("gq", [128, 128], F32, "ExternalInput")
    gk = k.dram("gk", [128, 128], F32, "ExternalInput")
    cos = k.dram("cos", [T, 512], F32, "ExternalInput")
    sin = k.dram("sin", [T, 512], F32, "ExternalInput")
    xn = k.dram("xn", [T, D], F32, "ExternalOutput")
    m_out = k.dram("m", [T, D], F32, "ExternalOutput")
    xT_d = k.dram("xT_d", [128, KC, T], BF16, "Internal")
    C = load_consts(k, cd)
    prologue(k, C, xa, [pa, pb], lnp, xn_out=xn, xb_out=None, xT_out=xT_d)
    gqa_body(k, C, xT_d, wq, wkv, wo, gq, gk, cos, sin, m_out)
    k.finish()
    return nc, es


def rope_tables(rot_dim):
    rows = T // 64
    row = np.repeat(np.arange(rows, dtype=np.int32), 64).astype(np.float32)
    col = np.tile(np.arange(64, dtype=np.int32), rows).astype(np.float32)
    half = rot_dim // 2
    inv = (np.float32(10000.0) ** (-np.arange(0, half, 2, dtype=np.float32) / np.float32(half))).astype(np.float32)
    ang = np.concatenate([row[:, None] * inv, col[:, None] * inv], axis=-1).astype(np.float32)
    return np.cos(ang).astype(np.float32), np.sin(ang).astype(np.float32)


def kc_layout(w):
    kc = w.shape[0] // 128
    return np.ascontiguousarray(w.reshape(kc, 128, w.shape[1]).transpose(1, 0, 2))


def gqa_inputs(i, layer, h, xa, pa, pb, inputs):
    w = inputs["gqa_w_qkv"][i]
    qcols = np.arange(8 * h * 128, (8 * h + 8) * 128)
    kcols = 2048 + np.arange(2 * h * 128, (2 * h + 2) * 128)
    vcols = 2048 + 512 + np.arange(2 * h * 128, (2 * h + 2) * 128)
    wo = inputs["gqa_w_out"][i][qcols, :]
    cos, sin = rope_tables(128)
    lnp = np.stack([np.broadcast_to(inputs["ln_ffn_g"][layer - 1], (128, D)),
                    np.broadcast_to(inputs["ln_ffn_b"][layer - 1], (128, D))]).astype(np.float32)
    return {"consts": consts_np(), "xa": xa, "pa": pa, "pb": pb, "lnp": np.ascontiguousarray(lnp),
            "wq": kc_layout(w[:, qcols]), "wkv": kc_layout(w[:, np.concatenate([kcols, vcols])]),
            "wo": kc_layout(wo),
            "gq": np.ascontiguousarray(np.broadcast_to(inputs["gqa_q_norm"][i], (128, 128))).astype(np.float32),
            "gk": np.ascontiguousarray(np.broadcast_to(inputs["gqa_k_norm"][i], (128, 128))).astype(np.float32),
            "cos": np.ascontiguousarray(np.tile(cos, (1, 8))), "sin": np.ascontiguousarray(np.tile(sin, (1, 8)))}


def widenorm(k, ps_ap, ps_buf, n, gtab, out_ap, out_buf, sq, rr):
    k.op("act", lambda e: e.activation(out=sq[:, 0:n], in_=ps_ap, func=AF.Square, accum_out=rr[:, 0:1]), [ps_buf], [sq, rr])
    k.op("dve", lambda e: e.tensor_scalar(out=rr[:, 1:2], in0=rr[:, 0:1], scalar1=1.0 / n, scalar2=RMS_EPS,
                                          op0=ALU.mult, op1=ALU.add), [rr], [rr])
    k.op("act", lambda e: e.activation(out=rr[:, 1:2], in_=rr[:, 1:2], func=AF.Sqrt), [rr], [rr])
    k.op("dve", lambda e: e.reciprocal(out=rr[:, 1:2], in_=rr[:, 1:2]), [rr], [rr])
    k.op("dve", lambda e: e.scalar_tensor_tensor(out=out_ap, in0=ps_ap, scalar=rr[:, 1:2], in1=gtab[:, 0:n],
                                                 op0=ALU.mult, op1=ALU.mult), [ps_buf, rr, gtab], [out_buf])


def mla_body(k, C, xT_d, W, cos_d, sin_d, ogT_d, m_out, use_gla=True):
    PS = k.psum
    k.mark()
    KnT = [k.sb(f"KnT{h}", [T], BF16) for h in range(4)]
    kpeT = k.sb("kpeT", [T], BF16)
    VA = k.sb("VA", [NT, 4, 130], BF16)
    C["on"] = k.sb("on", [128], BF16)
    C["rinv"] = k.sb("rinv", [4], F32)
    xTb = k.sb("xTb", [KC, 512], BF16)
    cos_t = [k.sb(f"cos{j}", [128], F32) for j in range(2)]
    sin_t = [k.sb(f"sin{j}", [128], F32) for j in range(2)]
    sq = k.sb("sq", [512], F32)
    rr = k.sb("rr", [8], F32)
    tmp = [k.sb(f"rtmp{j}", [128], F32) for j in range(2)]
    pe32 = k.sb("pe32", [256], F32)
    pebf = k.sb("pebf", [256], BF16)
    k.op("pool", lambda e: e.memset(VA[:, :, :, 128:130], 1.0), [], [VA])
    k.mark()
    wkvr = k.sb("wkvr", [KC, 320], BF16)
    wk = k.sb("wukv_k", [2, 512], BF16)
    wv = k.sb("wukv_v", [2, 512], BF16)
    gkv = k.sb("gkv", [256], F32)
    load_w_cast(k, wkvr, W["wkvr"].ap, W["wkvr"], 2)
    load_w_cast(k, wk, W["wukv_k"].ap, W["wukv_k"])
    load_w_cast(k, wv, W["wukv_v"].ap, W["wukv_v"])
    k.dma("sp", gkv[:, :], W["gkv"].ap, [W["gkv"]], [gkv])
    ckvn = k.sb("ckvn", [256], BF16)
    ckvnT = k.sb("ckvnT", [2, 512], BF16)
    for tb in range(8):
        k.dma("sp", xTb[:, :, :], xT_d.ap[:, :, tb * 512:(tb + 1) * 512], [xT_d], [xTb])
        for tl in range(4):
            ti = tb * 4 + tl
            j = ti % 2
            rows = slice(ti * 128, (ti + 1) * 128)
            k.dma("sp", cos_t[j][:, 0:32], cos_d.ap[rows, 0:32], [cos_d], [cos_t[j]])
            k.dma("sp", sin_t[j][:, 0:32], sin_d.ap[rows, 0:32], [sin_d], [sin_t[j]])
            ps = PS[6]
            for kc in range(KC):
                k.mm(ps[:, 0:320], xTb[:, kc, tl * 128:(tl + 1) * 128], wkvr[:, kc, :], kc == 0, kc == KC - 1, [xTb, wkvr], [ps])
            widenorm(k, ps[:, 0:256], ps, 256, gkv, ckvn[:, :], ckvn, sq, rr)
            k.op("act", lambda e, ps=ps: e.activation(out=pe32[:, 0:64], in_=ps[:, 256:320], func=AF.Copy), [ps], [pe32])
            rope_tm(k, (pe32, pe32[:, 0:64]), 1, 32, (cos_t[j], cos_t[j][:, 0:32]), (sin_t[j], sin_t[j][:, 0:32]),
                    (pebf, pebf[:, 0:64]), tmp)
            pst = PS[7]
            psb = pst.ap.bitcast(BF16)
            for c in range(2):
                k.tr(psb[:, c * 128:(c + 1) * 128], ckvn[:, c * 128:(c + 1) * 128], C["identb"][:, :], [ckvn, C["identb"]], [pst])
            k.tr(psb[0:64, 256:384], pebf[:, 0:64], C["identb"][:, :], [pebf, C["identb"]], [pst])
            k.op("act", lambda e, psb=psb, tl=tl: e.activation(
                out=ckvnT[:, :, tl * 128:(tl + 1) * 128], in_=psb[:, 0:256].rearrange("p (c t) -> p c t", c=2), func=AF.Copy),
                [pst], [ckvnT])
            k.op("dve", lambda e, psb=psb, rows=rows: e.tensor_copy(kpeT[0:64, rows], psb[0:64, 256:384]), [pst], [kpeT])
            ps2 = PS[6]
            for c in range(2):
                k.mm(ps2[:, :], ckvnT[:, c, tl * 128:(tl + 1) * 128], wv[:, c, :], c == 0, c == 1, [ckvnT, wv], [ps2])
            k.op("act", lambda e, ps2=ps2, ti=ti: e.activation(
                out=VA[:, ti, :, 0:128], in_=ps2[:, :].rearrange("p (h d) -> p h d", h=4), func=AF.Copy), [ps2], [VA])
        for hh in range(4):
            ps = PS[6]
            for c in range(2):
                k.mm(ps[:, :], wk[:, c, hh * 128:(hh + 1) * 128], ckvnT[:, c, :], c == 0, c == 1, [wk, ckvnT], [ps])
            k.op("dve", lambda e, ps=ps, hh=hh, tb=tb: e.tensor_copy(KnT[hh][:, tb * 512:(tb + 1) * 512], ps[:, :]),
                 [ps], [KnT[hh]])
    k.release()
    wcq = k.sb("wcq", [KC, 512], BF16)
    wqn = k.sb("wuq_n", [4, 512], BF16)
    wqp = k.sb("wuq_pe", [4, 256], BF16)
    wo = k.sb("wo", [8, D], BF16)
    gqn = k.sb("gqn", [512], F32)
    load_w_cast(k, wcq, W["wcq"].ap, W["wcq"], 2)
    load_w_cast(k, wqn, W["wuq_n"].ap, W["wuq_n"])
    load_w_cast(k, wqp, W["wuq_pe"].ap, W["wuq_pe"])
    load_w_cast(k, wo, W["wo"].ap, W["wo"], 4)
    k.dma("sp", gqn[:, :], W["gqn"].ap, [W["gqn"]], [gqn])
    cqn = k.sb("cqn", [512], BF16)
    cqnT = k.sb("cqnT", [4, 512], BF16)
    QnT = k.sb("QnT", [4, 512], BF16)
    QpT = k.sb("QpT", [4, 512], BF16)
    OT = k.sb("OT", [4, 512], BF16)
    ONb = k.sb("ONb", [4, 4, 128], BF16)
    ogb = k.sb("ogb", [4, 512], BF16)
    PT = [k.sb(f"PT{j}", [512], BF16) for j in range(3)]
    mt = k.sb("mt", [D], F32)
    scale = float(192 ** -0.5)
    for qb in range(8 if not DBG.get("skip_q") else 0):
        k.dma("sp", xTb[:, :, :], xT_d.ap[:, :, qb * 512:(qb + 1) * 512], [xT_d], [xTb])
        if use_gla:
            k.dma("sp", ogb[:, :, :], ogT_d.ap[:, :, qb * 512:(qb + 1) * 512], [ogT_d], [ogb])
        for tl in range(4):
            ti = qb * 4 + tl
            j = ti % 2
            rows = slice(ti * 128, (ti + 1) * 128)
            k.dma("sp", cos_t[j][:, :], cos_d.ap[rows, :], [cos_d], [cos_t[j]])
            k.dma("sp", sin_t[j][:, :], sin_d.ap[rows, :], [sin_d], [sin_t[j]])
            ps = PS[6]
            for kc in range(KC):
                k.mm(ps[:, :], xTb[:, kc, tl * 128:(tl + 1) * 128], wcq[:, kc, :], kc == 0, kc == KC - 1, [xTb, wcq], [ps])
            widenorm(k, ps[:, :], ps, 512, gqn, cqn[:, :], cqn, sq, rr)
            pst = PS[7]
            psb = pst.ap.bitcast(BF16)
            for c in range(4):
                k.tr(psb[:, c * 128:(c + 1) * 128], cqn[:, c * 128:(c + 1) * 128], C["identb"][:, :], [cqn, C["identb"]], [pst])
            k.op("act", lambda e, psb=psb, tl=tl: e.activation(
                out=cqnT[:, :, tl * 128:(tl + 1) * 128], in_=psb[:, 0:512].rearrange("p (c t) -> p c t", c=4), func=AF.Copy),
                [pst], [cqnT])
            ps2 = PS[6]
            for c in range(4):
                k.mm(ps2[:, 0:256], cqnT[:, c, tl * 128:(tl + 1) * 128], wqp[:, c, :], c == 0, c == 3, [cqnT, wqp], [ps2])
            k.op("act", lambda e, ps2=ps2: e.activation(out=pe32[:, :], in_=ps2[:, 0:256], func=AF.Copy), [ps2], [pe32])
            rope_tm(k, (pe32, pe32[:, :]), 4, 32, (cos_t[j], cos_t[j][:, :]), (sin_t[j], sin_t[j][:, :]), (pebf, pebf[:, :]), tmp)
            pst = PS[7]
            psb = pst.ap.bitcast(BF16)
            for hh in range(4):
                k.tr(psb[0:64, hh * 128:(hh + 1) * 128], pebf[:, hh * 64:(hh + 1) * 64], C["identb"][:, :],
                     [pebf, C["identb"]], [pst])
            k.op("dve", lambda e, psb=psb, tl=tl: e.tensor_copy(
                QpT[0:64, :, tl * 128:(tl + 1) * 128], psb[0:64, 0:512].rearrange("p (h t) -> p h t", h=4)), [pst], [QpT])
        for hh in range(4):
            ps = PS[6]
            for c in range(4):
                k.mm(ps[:, :], wqn[:, c, hh * 128:(hh + 1) * 128], cqnT[:, c, :], c == 0, c == 3, [wqn, cqnT], [ps])
            k.op("dve", lambda e, ps=ps, hh=hh: e.tensor_copy(QnT[:, hh, :], ps[:, :]), [ps], [QnT])
        for hh in range(4 if not DBG.get("skip_attn") else 0):
            C["ON_dst"] = (lambda sub, hh=hh: (ONb, ONb[:, hh, sub, :]))
            attn_head(k, C, [(KnT[hh], KnT[hh][:, :]), (kpeT, kpeT[0:64, :])],
                      [(QnT, QnT[:, hh, :]), (QpT, QpT[0:64, hh, :])],
                      lambda kc, hh=hh: (VA, VA[:, kc, hh, 0:129]), scale, PT,
                      lambda sub, hh=hh: (OT, OT[:, hh, sub * 128:(sub + 1) * 128]))
        C["ON_dst"] = None
        flush_on(k, C, ONb, 4, OT)
        if C.get("dbg_out") is not None and qb == 7:
            dbt = k.sb("dbt", [2048], F32)
            k.op("dve", lambda e: e.tensor_copy(dbt[:, :], ONb[:, :, :, :].rearrange("p a b c -> p (a b c)")), [ONb], [dbt])
            k.dma("sp", C["dbg_out"].ap, dbt[:, :], [dbt], [C["dbg_out"]])
        nsrc = 8 if use_gla else 4
        for tl in range(4):
            ti = qb * 4 + tl
            for cb in range(4):
                ps = PS[6]
                for s_ in range(nsrc):
                    src, sbuf = (OT[:, s_, tl * 128:(tl + 1) * 128], OT) if s_ < 4 else \
                        (ogb[:, s_ - 4, tl * 128:(tl + 1) * 128], ogb)
                    k.mm(ps[:, :], src, wo[:, s_, cb * 512:(cb + 1) * 512], s_ == 0, s_ == nsrc - 1, [sbuf, wo], [ps])
                k.op("dve", lambda e, ps=ps, cb=cb: e.tensor_copy(mt[:, cb * 512:(cb + 1) * 512], ps[:, :]), [ps], [mt])
            k.dma("sp", m_out.ap[ti * 128:(ti + 1) * 128, :], mt[:, :], [mt], [m_out])
    k.release()


GLA_W = [("wg_qk", [128, KC, 512]), ("wg_lat", [128, KC, 32]), ("wg_v", [128, KC, 512]), ("wg_r", [128, KC, 512]),
         ("gw2", [16, 2, 256]), ("gnb", [128, 4]), ("gon", [128, 512])]


def gla_body(k, C, cd, xT_d, G, ogT_d):
    PS = k.psum
    k.mark()
    maskF = k.sb("maskF", [128], F32)
    maskB = k.sb("maskB", [128], F32)
    k.dma("sp", maskF[:, :], cd.ap[:, 640:768], [cd], [maskF])
    k.dma("sp", maskB[:, :], cd.ap[:, 768:896], [cd], [maskB])
    wqk = k.sb("wg_qk", [KC, 512], BF16)
    wlat = k.sb("wg_lat", [KC, 32], BF16)
    wv = k.sb("wg_v", [KC, 512], BF16)
    wr = k.sb("wg_r", [KC, 512], BF16)
    gw2 = k.sb("gw2", [2, 256], BF16)
    gnb = k.sb("gnb", [4], F32)
    gon = k.sb("gon", [512], F32)
    load_w_cast(k, wqk, G["wg_qk"].ap, G["wg_qk"], 2)
    load_w_cast(k, wlat, G["wg_lat"].ap, G["wg_lat"])
    load_w_cast(k, wv, G["wg_v"].ap, G["wg_v"], 2)
    load_w_cast(k, wr, G["wg_r"].ap, G["wg_r"], 2)
    k.dma("pool", gw2[0:16, :, :], G["gw2"].ap, [G["gw2"]], [gw2])
    k.dma("sp", gnb[:, :], G["gnb"].ap, [G["gnb"]], [gnb])
    k.op("dve", lambda e: e.tensor_scalar(out=gnb[:, :], in0=gnb[:, :], scalar1=-1.0, scalar2=None, op0=ALU.mult), [gnb], [gnb])
    k.dma("sp", gon[:, :], G["gon"].ap, [G["gon"]], [gon])
    SBst = k.sb("SBst", [64, 2, 256], BF16)
    S32 = [k.sb(f"S32_{h}", [256], F32) for h in range(2)]
    Sbf = [[k.sb(f"Sbf_{h}_{j}", [256], BF16) for j in range(2)] for h in range(2)]
    xTb = k.sb("xTb", [KC, 512], BF16)
    gk32 = [k.sb(f"gk32_{h}", [512], F32) for h in range(2)]
    gq32 = [k.sb(f"gq32_{h}", [512], F32) for h in range(2)]
    lb = k.sb("l", [512], F32)
    cum = k.sb("cum", [512], F32)
    c2b = k.sb("c2", [512], F32)
    E1 = k.sb("E1", [512], F32)
    E2 = k.sb("E2", [512], F32)
    kin32 = k.sb("kin32", [512], F32)
    kstT = k.sb("kstT", [512], BF16)
    qin = [[k.sb(f"qin_{h}_{d}", [512], BF16) for d in range(2)] for h in range(2)]
    kin = [[k.sb(f"kin_{h}_{d}", [512], BF16) for d in range(2)] for h in range(2)]
    dec = [[k.sb(f"dec_{h}_{d}", [8], F32) for d in range(2)] for h in range(2)]
    latT = [k.sb(f"latT{d}", [512], BF16) for d in range(2)]
    kstTM = [k.sb(f"kstTM{h}", [4, 128], BF16) for h in range(2)]
    gvTM = k.sb("gvTM", [4, 512], BF16)
    grT = k.sb("grT", [4, 512], F32)
    at1 = k.sb("at1", [128], F32)
    at2 = k.sb("at2", [128], F32)
    att = k.sb("att", [128], BF16)
    sq = k.sb("gsq", [256], F32)
    rr = k.sb("grr", [4], F32)
    onb = k.sb("onb", [256], BF16)
    ogTb = k.sb("ogTb", [4, 512], BF16)
    ones64 = C["onesb"]

    def proj_fm(ps, wbuf, c0, m, dst_fn, reads_extra=()):
        for kc in range(KC):
            k.mm(ps[0:m, :], wbuf[:, kc, c0:c0 + m], xTb[:, kc, :], kc == 0, kc == KC - 1, [wbuf, xTb], [ps])
        dst_fn(ps)

    def prep(hd, dr, want_q, want_kst):
        ps = PS[2]
        k.mm(ps[:, :], gw2[0:16, dr, hd * 128:(hd + 1) * 128], latT[dr][0:16, :], True, True, [gw2, latT[dr]], [ps])
        col = dr * 2 + hd
        k.op("act", lambda e: e.activation(out=lb[:, :], in_=ps[:, :], func=AF.Exp, scale=-1.0, bias=gnb[:, col:col + 1]),
             [ps, gnb], [lb])
        k.op("act", lambda e: e.activation(out=lb[:, :], in_=lb[:, :], func=AF.Ln, bias=1.0, scale=1.0), [lb], [lb])
        for ch in range(8):
            cs_ = slice(ch * 64, (ch + 1) * 64)
            k.op("dve", lambda e, cs_=cs_: e.tensor_tensor_scan(out=cum[:, cs_], data0=ones64[:, 0:64], data1=lb[:, cs_],
                                                                 initial=0.0, op0=ALU.mult, op1=ALU.add), [lb, ones64], [cum])
        if dr == 0:
            cc = cum
        else:
            for ch in range(8):
                cs_ = slice(ch * 64, (ch + 1) * 64)
                k.op("dve", lambda e, cs_=cs_, ch=ch: e.tensor_scalar(
                    out=c2b[:, cs_], in0=cum[:, cs_], scalar1=cum[:, ch * 64 + 63:ch * 64 + 64], scalar2=-1.0,
                    op0=ALU.subtract, op1=ALU.mult), [cum], [c2b])
            k.op("dve", lambda e: e.tensor_tensor(out=c2b[:, :], in0=c2b[:, :], in1=lb[:, :], op=ALU.add), [c2b, lb], [c2b])
            cc = c2b
        k.op("act", lambda e: e.activation(out=E1[:, :], in_=cc[:, :], func=AF.Exp, scale=-1.0 / 16), [cc], [E1])
        k.op("act", lambda e: e.activation(out=E2[:, :], in_=cc[:, :], func=AF.Exp, scale=1.0 / 16), [cc], [E2])
        dcol = 63 if dr == 0 else 0
        d_ = dec[hd][dr]
        k.op("dve", lambda e: e.tensor_copy(d_[:, 0:8], E1[:, :].rearrange("p (c t) -> p c t", t=64)[:, :, dcol]), [E1], [d_])
        k.op("dve", lambda e: e.tensor_tensor(out=kin32[:, :], in0=gk32[hd][:, :], in1=E2[:, :], op=ALU.mult),
             [gk32[hd], E2], [kin32])
        if want_q:
            k.op("act", lambda e: e.activation(out=kin[hd][dr][:, :], in_=kin32[:, :], func=AF.Copy), [kin32], [kin[hd][dr]])
            k.op("pool", lambda e: e.tensor_tensor(out=qin[hd][dr][:, :], in0=gq32[hd][:, :], in1=E1[:, :], op=ALU.mult),
                 [gq32[hd], E1], [qin[hd][dr]])
        if want_kst:
            for ch in range(8):
                cs_ = slice(ch * 64, (ch + 1) * 64)
                k.op("dve", lambda e, cs_=cs_, ch=ch: e.tensor_scalar(out=kstT[:, cs_], in0=kin32[:, cs_],
                                                                      scalar1=d_[:, ch:ch + 1], scalar2=None, op0=ALU.mult),
                     [kin32, d_], [kstT])
            pst = PS[4]
            psb = pst.ap.bitcast(BF16)
            for tl in range(4):
                k.tr(psb[:, tl * 128:(tl + 1) * 128], kstT[:, tl * 128:(tl + 1) * 128], C["identb"][:, :],
                     [kstT, C["identb"]], [pst])
            k.op("act", lambda e: e.activation(out=kstTM[hd][:, :, :], in_=psb[:, 0:512].rearrange("p (a b) -> p a b", a=4),
                                               func=AF.Copy), [pst], [kstTM[hd]])

    def common_proj(tb, fwd):
        k.dma("sp", xTb[:, :, :], xT_d.ap[:, :, tb * 512:(tb + 1) * 512], [xT_d], [xTb])
        for hd in range(2):
            proj_fm(PS[hd], wqk, 256 + hd * 128, 128,
                    lambda ps, hd=hd: k.op("act", lambda e: e.activation(out=gk32[hd][:, :], in_=ps[:, :], func=AF.Copy),
                                           [ps], [gk32[hd]]))
        for dr in ([0, 1] if fwd else [1]):
            proj_fm(PS[2], wlat, dr * 16, 16,
                    lambda ps, dr=dr: k.op("dve", lambda e: e.tensor_copy(latT[dr][0:16, :], ps[0:16, :]), [ps], [latT[dr]]))
        for tl in range(4):
            ps = PS[3]
            for kc in range(KC):
                k.mm(ps[:, :], xTb[:, kc, tl * 128:(tl + 1) * 128], wv[:, kc, :], kc == 0, kc == KC - 1, [xTb, wv], [ps])
            k.op("act", lambda e, ps=ps, tl=tl: e.activation(out=gvTM[:, tl, :], in_=ps[:, :], func=AF.Copy), [ps], [gvTM])

    def state_update(hd, dr, n_local):
        tl = n_local // 2
        r0 = (n_local % 2) * 64
        ps = PS[7]
        k.mm(ps[:, 0:256], kstTM[hd][r0:r0 + 64, tl, :], gvTM[r0:r0 + 64, tl, hd * 256:(hd + 1) * 256], True, True,
             [kstTM[hd], gvTM], [ps])
        d_ = dec[hd][dr]
        k.op("dve", lambda e: e.scalar_tensor_tensor(out=S32[hd][:, :], in0=S32[hd][:, :], scalar=d_[:, n_local:n_local + 1],
                                                     in1=ps[:, 0:256], op0=ALU.mult, op1=ALU.add), [S32[hd], d_, ps], [S32[hd]])

    for hd in range(2):
        k.op("dve", lambda e, hd=hd: e.memset(S32[hd][:, :], 0.0), [], [S32[hd]])
    _bst = 9
    for tb in range(7, -1, -1) if _bst >= 1 else []:
        common_proj(tb, False)
        for hd in range(2) if _bst >= 2 else []:
            prep(hd, 1, False, True)
        for n_local in range(7, -1, -1) if _bst >= 3 else []:
            n = tb * 8 + n_local
            for hd in range(2):
                k.op("act", lambda e, hd=hd, n=n: e.activation(out=SBst[:, n, hd, :], in_=S32[hd][:, :], func=AF.Copy),
                     [S32[hd]], [SBst])
                state_update(hd, 1, n_local)
    _stage = 9
    for hd in range(2):
        k.op("dve", lambda e, hd=hd: e.memset(S32[hd][:, :], 0.0), [], [S32[hd]])
        k.op("pool", lambda e, hd=hd: e.memset(Sbf[hd][0][:, :], 0.0), [], [Sbf[hd][0]])
    cur = [0, 0]
    if _stage < 4:
        k.op("pool", lambda e: e.memset(ogTb[:, :, :], 0.0), [], [ogTb])
    if _stage < 2:
        for tb in range(8):
            k.dma("sp", ogT_d.ap[:, :, tb * 512:(tb + 1) * 512], ogTb[:, :, :], [ogTb], [ogT_d])
    for tb in range(8 if _stage >= 2 else 0):
        common_proj(tb, True)
        for hd in range(2):
            proj_fm(PS[hd], wqk, hd * 128, 128,
                    lambda ps, hd=hd: k.op("act", lambda e: e.activation(out=gq32[hd][:, :], in_=ps[:, :], func=AF.Copy,
                                                                         scale=float(128 ** -0.5)), [ps], [gq32[hd]]))
        for c in range(4):
            proj_fm(PS[3], wr, c * 128, 128,
                    lambda ps, c=c: k.op("act", lambda e: e.activation(out=grT[:, c, :], in_=ps[:, :], func=AF.Silu),
                                         [ps], [grT]))
        for hd in range(2):
            prep(hd, 1, True, False)
            prep(hd, 0, True, True)
        for tl in range(4 if _stage >= 3 else 0):
            tc_ = slice(tl * 128, (tl + 1) * 128)
            for hd in range(2):
                p5 = PS[5]
                k.mm(p5[:, 0:128], kin[hd][0][:, tc_], qin[hd][0][:, tc_], True, True, [kin[hd][0], qin[hd][0]], [p5])
                k.mm(p5[:, 128:256], kin[hd][1][:, tc_], qin[hd][1][:, tc_], True, True, [kin[hd][1], qin[hd][1]], [p5])
                k.op("dve", lambda e, p5=p5: e.tensor_tensor(out=at1[:, :], in0=p5[:, 0:128], in1=maskF[:, :], op=ALU.mult),
                     [p5, maskF], [at1])
                k.op("dve", lambda e, p5=p5: e.tensor_tensor(out=at2[:, :], in0=p5[:, 128:256], in1=maskB[:, :], op=ALU.mult),
                     [p5, maskB], [at2])
                k.op("pool", lambda e: e.tensor_tensor(out=att[:, :], in0=at1[:, :], in1=at2[:, :], op=ALU.add),
                     [at1, at2], [att])
                for c2 in range(2 if _stage >= 4 else 0):
                    n_local = tl * 2 + c2
                    n = tb * 8 + n_local
                    r0 = c2 * 64
                    cc_ = slice(tl * 128 + r0, tl * 128 + r0 + 64)
                    p6 = PS[6]
                    sb_ = Sbf[hd][cur[hd]]
                    k.mm(p6[0:64, 0:256], att[r0:r0 + 64, r0:r0 + 64], gvTM[r0:r0 + 64, tl, hd * 256:(hd + 1) * 256],
                         True, False, [att, gvTM], [p6])
                    k.mm(p6[0:64, 0:256], qin[hd][0][:, cc_], sb_[:, :], False, False, [qin[hd][0], sb_], [p6])
                    k.mm(p6[0:64, 0:256], qin[hd][1][:, cc_], SBst[:, n, hd, :], False, True, [qin[hd][1], SBst], [p6])
                    R64 = slice(0, 64)
                    k.op("act", lambda e, p6=p6: e.activation(out=sq[R64, :], in_=p6[R64, 0:256], func=AF.Square,
                                                              accum_out=rr[R64, 0:1]), [p6], [sq, rr])
                    k.op("dve", lambda e: e.tensor_scalar(out=rr[R64, 1:2], in0=rr[R64, 0:1], scalar1=1.0 / 256, scalar2=RMS_EPS,
                                                          op0=ALU.mult, op1=ALU.add), [rr], [rr])
                    k.op("act", lambda e: e.activation(out=rr[R64, 1:2], in_=rr[R64, 1:2], func=AF.Sqrt), [rr], [rr])
                    k.op("dve", lambda e: e.reciprocal(out=rr[R64, 1:2], in_=rr[R64, 1:2]), [rr], [rr])
                    k.op("dve", lambda e, p6=p6, hd=hd: e.scalar_tensor_tensor(
                        out=onb[R64, :], in0=p6[R64, 0:256], scalar=rr[R64, 1:2], in1=gon[R64, hd * 256:(hd + 1) * 256],
                        op0=ALU.mult, op1=ALU.mult), [p6, rr, gon], [onb])
                    pst = PS[4]
                    psb = pst.ap.bitcast(BF16)
                    for c in range(2):
                        k.tr(psb[:, c * 64:(c + 1) * 64], onb[R64, c * 128:(c + 1) * 128], C["identb"][R64, 0:64],
                             [onb, C["identb"]], [pst])
                    for c in range(2):
                        k.op("dve", lambda e, c=c, psb=psb, hd=hd, cc_=cc_: e.tensor_tensor(
                            out=ogTb[:, hd * 2 + c, cc_], in0=psb[:, c * 64:(c + 1) * 64], in1=grT[:, hd * 2 + c, cc_],
                            op=ALU.mult), [pst, grT], [ogTb])
                    state_update(hd, 0, n_local)
                    cur[hd] ^= 1
                    nb_ = Sbf[hd][cur[hd]]
                    k.op("act", lambda e, nb_=nb_, hd=hd: e.activation(out=nb_[:, :], in_=S32[hd][:, :], func=AF.Copy),
                         [S32[hd]], [nb_])
        k.dma("sp", ogT_d.ap[:, :, tb * 512:(tb + 1) * 512], ogTb[:, :, :], [ogTb], [ogT_d])
    k.release()


EVEN_W = [("wkvr", [128, KC, 320]), ("wukv_k", [128, 2, 512]), ("wukv_v", [128, 2, 512]), ("wcq", [128, KC, 512]),
          ("wuq_n", [128, 4, 512]), ("wuq_pe", [128, 4, 256]), ("wo", [128, 8, D]), ("gkv", [128, 256]), ("gqn", [128, 512])]


def build_even(first, use_gla=True, use_mla=True, mla_reads_gla=True):
    nc = bass.Bass("TRN2", target_bir_lowering=False)
    es = ExitStack()
    k = K(nc, es)
    cd = k.dram("consts", [128, 896], F32, "ExternalInput")
    xa = k.dram("xa", [T, D], F32, "ExternalInput")
    parts = []
    lnp = None
    if not first:
        parts = [k.dram("pa", [T, D], F32, "ExternalInput"), k.dram("pb", [T, D], F32, "ExternalInput")]
        lnp = k.dram("lnp", [2, 128, D], F32, "ExternalInput")
    W = {n: k.dram(n, s, F32, "ExternalInput") for n, s in EVEN_W}
    G = {n: k.dram(n, s, F32, "ExternalInput") for n, s in GLA_W} if use_gla else None
    cos = k.dram("cos", [T, 128], F32, "ExternalInput")
    sin = k.dram("sin", [T, 128], F32, "ExternalInput")
    xn = k.dram("xn", [T, D], F32, "ExternalOutput")
    m_out = k.dram("m", [T, D], F32, "ExternalOutput")
    xT_d = k.dram("xT_d", [128, KC, T], BF16, "Internal")
    ogT_d = k.dram("ogT_d", [128, 4, T], BF16, "Internal")
    C = load_consts(k, cd)
    prologue(k, C, xa, parts, lnp, xn_out=xn, xb_out=None, xT_out=xT_d)
    if use_gla:
        gla_body(k, C, cd, xT_d, G, ogT_d)
    mla_body(k, C, xT_d, W, cos, sin, ogT_d, m_out, use_gla=use_gla and mla_reads_gla)
    k.finish()
    return nc, es


def consts_even_np():
    c = np.zeros((128, 896), np.float32)
    c[:, 0:640] = consts_np()
    jj = np.arange(128)[:, None]
    ii = np.arange(128)[None, :]
    same = (jj // 64) == (ii // 64)
    c[:, 640:768] = (same & (jj <= ii)).astype(np.float32)
    c[:, 768:896] = (same & (jj >= ii)).astype(np.float32)
    return c


def even_inputs(i, layer, h, xa, pa, pb, inputs):
    w_in = inputs["mix_w_in"][i]
    heads = list(range(4 * h, 4 * h + 4))
    wuq = inputs["mla_w_uq"][i]
    wukv = inputs["mla_w_ukv"][i]
    ncols = np.concatenate([np.arange(hh * 192, hh * 192 + 128) for hh in heads])
    pcols = np.concatenate([np.arange(hh * 192 + 128, hh * 192 + 192) for hh in heads])
    kcols = np.concatenate([np.arange(hh * 256, hh * 256 + 128) for hh in heads])
    vcols = np.concatenate([np.arange(hh * 256 + 128, hh * 256 + 256) for hh in heads])
    wo_rows = np.concatenate([np.arange(4 * h * 128, (4 * h + 4) * 128), 1024 + np.arange(2 * h * 256, (2 * h + 2) * 256)])
    cos, sin = rope_tables(64)
    bc = lambda v, n: np.ascontiguousarray(np.broadcast_to(v, (128, n))).astype(np.float32)
    m = {"consts": consts_even_np(), "xa": xa,
         "wkvr": kc_layout(w_in[:, 512:832]), "wukv_k": kc_layout(wukv[:, kcols]), "wukv_v": kc_layout(wukv[:, vcols]),
         "wcq": kc_layout(w_in[:, 0:512]), "wuq_n": kc_layout(wuq[:, ncols]), "wuq_pe": kc_layout(wuq[:, pcols]),
         "wo": kc_layout(inputs["mix_w_out"][i][wo_rows, :]),
         "gkv": bc(inputs["mla_kv_norm"][i], 256), "gqn": bc(inputs["mla_q_norm"][i], 512),
         "cos": np.ascontiguousarray(np.tile(cos, (1, 4))), "sin": np.ascontiguousarray(np.tile(sin, (1, 4)))}
    gh = [2 * h, 2 * h + 1]
    qc = np.concatenate([832 + g * 128 + np.arange(128) for g in gh])
    kcs = np.concatenate([1344 + g * 128 + np.arange(128) for g in gh])
    vc = np.concatenate([1856 + g * 256 + np.arange(256) for g in gh])
    rc = np.concatenate([2880 + g * 256 + np.arange(256) for g in gh])
    dkc = np.concatenate([g * 128 + np.arange(128) for g in gh])
    dvc = np.concatenate([g * 256 + np.arange(256) for g in gh])
    gw2 = inputs["gla_gate_w2"][i][:, :, dkc]
    gb = inputs["gla_gate_b"][i][:, dkc]
    gnb = np.stack([gb[0, 0:128], gb[0, 128:256], gb[1, 0:128], gb[1, 128:256]], axis=1)
    m.update({"wg_qk": kc_layout(w_in[:, np.concatenate([qc, kcs])]), "wg_lat": kc_layout(w_in[:, 3904:3936]),
              "wg_v": kc_layout(w_in[:, vc]), "wg_r": kc_layout(w_in[:, rc]),
              "gw2": np.ascontiguousarray(gw2.transpose(1, 0, 2)), "gnb": np.ascontiguousarray(gnb.astype(np.float32)),
              "gon": bc(inputs["gla_out_norm"][i][dvc], 512)})
    if layer > 0:
        lnp = np.stack([np.broadcast_to(inputs["ln_ffn_g"][layer - 1], (128, D)),
                        np.broadcast_to(inputs["ln_ffn_b"][layer - 1], (128, D))]).astype(np.float32)
        m.update({"pa": pa, "pb": pb, "lnp": np.ascontiguousarray(lnp)})
    return m


def build_gla(first):
    nc = bass.Bass("TRN2", target_bir_lowering=False)
    es = ExitStack()
    k = K(nc, es)
    cd = k.dram("consts", [128, 896], F32, "ExternalInput")
    xa = k.dram("xa", [T, D], F32, "ExternalInput")
    parts = []
    lnp = None
    if not first:
        parts = [k.dram("pa", [T, D], F32, "ExternalInput"), k.dram("pb", [T, D], F32, "ExternalInput")]
        lnp = k.dram("lnp", [2, 128, D], F32, "ExternalInput")
    G = {n: k.dram(n, s, F32, "ExternalInput") for n, s in GLA_W}
    xn = k.dram("xn", [T, D], F32, "ExternalOutput")
    ogT = k.dram("ogT", [128, 4, T], BF16, "ExternalOutput")
    xT_d = k.dram("xT_d", [128, KC, T], BF16, "Internal")
    C = load_consts(k, cd)
    prologue(k, C, xa, parts, lnp, xn_out=xn, xb_out=None, xT_out=xT_d)
    gla_body(k, C, cd, xT_d, G, ogT)
    k.finish()
    return nc, es


def build_mla():
    nc = bass.Bass("TRN2", target_bir_lowering=False)
    es = ExitStack()
    k = K(nc, es)
    cd = k.dram("consts", [128, 896], F32, "ExternalInput")
    xa = k.dram("xa", [T, D], F32, "ExternalInput")
    W = {n: k.dram(n, s, F32, "ExternalInput") for n, s in EVEN_W}
    cos = k.dram("cos", [T, 128], F32, "ExternalInput")
    sin = k.dram("sin", [T, 128], F32, "ExternalInput")
    ogT = k.dram("ogT", [128, 4, T], BF16, "ExternalInput")
    m_out = k.dram("m", [T, D], F32, "ExternalOutput")
    xT_d = k.dram("xT_d", [128, KC, T], BF16, "Internal")
    C = load_consts(k, cd)
    prologue(k, C, xa, [], None, xn_out=None, xb_out=None, xT_out=xT_d)
    mla_body(k, C, xT_d, W, cos, sin, ogT, m_out, use_gla=True)
    k.finish()
    return nc, es


def build_final():
    nc = bass.Bass("TRN2", target_bir_lowering=False)
    es = ExitStack()
    k = K(nc, es)
    cd = k.dram("consts", [128, 640], F32, "ExternalInput")
    xa = k.dram("xa", [T, D], F32, "ExternalInput")
    pa = k.dram("pa", [T, D], F32, "ExternalInput")
    pb = k.dram("pb", [T, D], F32, "ExternalInput")
    lnp = k.dram("lnp", [2, 128, D], F32, "ExternalInput")
    xn = k.dram("xn", [T, D], F32, "ExternalOutput")
    C = load_consts(k, cd)
    prologue(k, C, xa, [pa, pb], lnp, xn_out=xn, xb_out=None, xT_out=None)
    k.finish()
    return nc, es


def _lnp(g, b):
    return np.ascontiguousarray(np.stack([np.broadcast_to(g, (128, D)), np.broadcast_to(b, (128, D))]).astype(np.float32))


_PROGS = {}


def _prog(name, fn):
    if name not in _PROGS:
        _PROGS[name] = fn()
    return _PROGS[name][0]


def _run(nc, maps):
    res = run_bass_kernel_spmd(nc, maps, core_ids=list(range(8)))
    return res.results


def kernel(**inputs):
    inputs = {k_: np.asarray(v) for k_, v in inputs.items()}
    Bn = 4
    x = [np.ascontiguousarray(inputs["x"][b]) for b in range(Bn)]
    pa = [None] * Bn
    pb = [None] * Bn
    GK = [n for n, _ in GLA_W]
    for layer in range(4):
        i = layer // 2
        first = layer == 0
        if layer % 2 == 0:
            maps_full = [even_inputs(i, layer, c % 2, x[c // 2], pa[c // 2], pb[c // 2], inputs) for c in range(8)]
            gkeys = ["consts", "xa"] + GK + ([] if first else ["pa", "pb", "lnp"])
            nc = _prog("gla_first" if first else "gla", (lambda: build_gla(True)) if first else (lambda: build_gla(False)))
            r = _run(nc, [{k_: m[k_] for k_ in gkeys} for m in maps_full])
            xn = [np.asarray(r[2 * b]["xn"]) for b in range(Bn)]
            ogT = [np.asarray(r[c]["ogT"]) for c in range(8)]
            mkeys = ["consts"] + [n for n, _ in EVEN_W] + ["cos", "sin"]
            nc = _prog("mla", build_mla)
            maps = []
            for c in range(8):
                m = {k_: maps_full[c][k_] for k_ in mkeys}
                m["xa"] = xn[c // 2]
                m["ogT"] = ogT[c]
                maps.append(m)
            del maps_full
            r = _run(nc, maps)
            mm_ = [np.asarray(r[c]["m"]) for c in range(8)]
        else:
            maps = [gqa_inputs(i, layer, c % 2, x[c // 2], pa[c // 2], pb[c // 2], inputs) for c in range(8)]
            nc = _prog("gqa", build_gqa)
            r = _run(nc, maps)
            xn = [np.asarray(r[2 * b]["xn"]) for b in range(Bn)]
            mm_ = [np.asarray(r[c]["m"]) for c in range(8)]
        del maps
        nc = _prog("moe", build_moe)
        base = [moe_inputs(layer, h_, None, None, None, inputs) for h_ in range(2)]
        maps = []
        for c in range(8):
            m = dict(base[c % 2])
            m["xa"] = xn[c // 2]
            m["pa"] = mm_[2 * (c // 2)]
            m["pb"] = mm_[2 * (c // 2) + 1]
            maps.append(m)
        del base
        r = _run(nc, maps)
        del maps
        x = [np.asarray(r[2 * b]["xmid"]) for b in range(Bn)]
        pa = [np.asarray(r[2 * b]["f"]) for b in range(Bn)]
        pb = [np.asarray(r[2 * b + 1]["f"]) for b in range(Bn)]
    nc = _prog("final", build_final)
    lnp = _lnp(inputs["ln_ffn_g"][3], inputs["ln_ffn_b"][3])
    maps = [{"consts": consts_np(), "xa": x[c // 2], "pa": pa[c // 2], "pb": pb[c // 2], "lnp": lnp} for c in range(8)]
    r = _run(nc, maps)
    out = np.stack([np.asarray(r[2 * b]["xn"]) for b in range(Bn)]).astype(np.float32)
    return out
```
